# Optimizing a Trainium2 kernel written in Bass

```python
import math
import jax
import jax.numpy as jnp
from jax import lax
import numpy as np

D_MODEL = 1024
BATCH = 8
SEQ = 8192
DEPTH = 1

MIX_WIDTH = D_MODEL
PLE_DIM = 256
DA_WIDTH = MIX_WIDTH // 2
DA_HEADS = 4
DA_V_DIM = DA_WIDTH // DA_HEADS
DA_HEAD_DIM = DA_V_DIM // 2
ROT_DIM = DA_HEAD_DIM // 4
ROPE_THETA = 500000.0
Q_BLOCK = 128
HG_WIDTH = MIX_WIDTH - DA_WIDTH
HG_HEADS = 4
HG_KEY_DIM = HG_WIDTH // HG_HEADS
HG_VAL_DIM = HG_WIDTH // HG_HEADS
CHUNK = 64
D_FF = 2816
CONV_WIDTH = 3
EPS = 1e-6

IN_SPLIT_SIZES = (
    DA_HEADS * 2 * DA_HEAD_DIM,
    DA_HEADS * 2 * DA_HEAD_DIM,
    DA_HEADS * DA_V_DIM,
    HG_HEADS * HG_KEY_DIM,
    HG_HEADS * HG_KEY_DIM,
    HG_HEADS * HG_KEY_DIM,
    HG_HEADS * HG_VAL_DIM,
    HG_HEADS * HG_VAL_DIM,
)
IN_COLS = sum(IN_SPLIT_SIZES)
IN_SPLIT_POINTS = tuple(int(c) for c in np.cumsum(IN_SPLIT_SIZES)[:-1])

kernel_name = "hybrid_diffattn_hgrn2_convglu_encoder"


def rmsnorm(x, g, eps=EPS):
    xf = x.astype(jnp.float32)
    y = xf * lax.rsqrt(jnp.mean(xf * xf, axis=-1, keepdims=True) + eps)
    return (y * g.astype(jnp.float32)).astype(x.dtype)


def partial_rotary(t, positions):
    half = ROT_DIM // 2
    inv_freq = ROPE_THETA ** (-jnp.arange(half, dtype=jnp.float32) / half)
    ang = positions.astype(jnp.float32)[..., None] * inv_freq
    cos = jnp.cos(ang)[:, :, None, None, :]
    sin = jnp.sin(ang)[:, :, None, None, :]
    tr = t[..., :ROT_DIM].astype(jnp.float32)
    x1, x2 = tr[..., :half], tr[..., half:]
    rot = jnp.concatenate([x1 * cos - x2 * sin, x2 * cos + x1 * sin], axis=-1)
    return jnp.concatenate([rot.astype(t.dtype), t[..., ROT_DIM:]], axis=-1)


def diff_attention(q, k, v, lam_q1, lam_k1, lam_q2, lam_k2, subln_g, lambda_init):
    B, S = q.shape[0], q.shape[1]
    f32 = jnp.float32
    lam = (jnp.exp(jnp.sum(lam_q1.astype(f32) * lam_k1.astype(f32)))
           - jnp.exp(jnp.sum(lam_q2.astype(f32) * lam_k2.astype(f32))) + lambda_init)
    scale = DA_HEAD_DIM ** -0.5
    qb = (q * scale).reshape(B, S // Q_BLOCK, Q_BLOCK, DA_HEADS, 2, DA_HEAD_DIM)
    qb = qb.transpose(1, 0, 2, 3, 4, 5)

    def block(qblk):
        s = jnp.einsum('bqhmd,bkhmd->bhmqk', qblk, k, preferred_element_type=f32)
        a = jax.nn.softmax(s, axis=-1)
        w = a[:, :, 0] - lam * a[:, :, 1]
        return jnp.einsum('bhqk,bkhe->bqhe', w.astype(v.dtype), v)

    o = lax.map(block, qb)
    o = o.transpose(1, 0, 2, 3, 4).reshape(B, S, DA_HEADS, DA_V_DIM)
    o = rmsnorm(o, subln_g) * (1.0 - lambda_init)
    return o.reshape(B, S, DA_HEADS * DA_V_DIM)


def _to_chunks(t):
    z, b, s, h, e = t.shape
    return t.reshape(z, b, s // CHUNK, CHUNK, h, e).transpose(2, 0, 1, 4, 3, 5)


def hgrn2_bidirectional(q, zf, zb, v, g, lb_gamma, norm_g, layer):
    B, S = q.shape[0], q.shape[1]
    f32 = jnp.float32
    H, dk, dv = HG_HEADS, HG_KEY_DIM, HG_VAL_DIM
    lb_all = jnp.cumsum(jax.nn.softmax(lb_gamma.astype(f32), axis=1), axis=1)
    lb = lb_all[:, layer].reshape(2, 1, 1, H, dk)
    q4 = q.reshape(B, S, H, dk).astype(f32)
    v4 = v.reshape(B, S, H, dv).astype(f32)
    z = jnp.stack([zf.reshape(B, S, H, dk), zb.reshape(B, S, H, dk)[:, ::-1]], 0).astype(f32)
    log_f = jnp.log(lb + (1.0 - lb) * jax.nn.sigmoid(z))
    k_in = (1.0 - lb) * jax.nn.sigmoid(-z)
    qd = jnp.stack([q4, q4[:, ::-1]], 0)
    vd = jnp.stack([v4, v4[:, ::-1]], 0)
    xs = (_to_chunks(qd), _to_chunks(k_in), _to_chunks(vd), _to_chunks(log_f))
    tri = jnp.tril(jnp.ones((CHUNK, CHUNK), dtype=bool))[:, :, None]

    def step(state, inp):
        qc, kc, vc, gc = inp
        bcum = jnp.cumsum(gc, axis=-2)
        b_last = bcum[..., -1:, :]
        o_inter = jnp.einsum('zbhtk,zbhkv->zbhtv', qc * jnp.exp(bcum), state)
        diff = bcum[..., :, None, :] - bcum[..., None, :, :]
        decay = jnp.exp(jnp.where(tri, diff, -jnp.inf))
        attn = jnp.einsum('zbhtk,zbhtsk->zbhts', qc, decay * kc[..., None, :, :])
        o_intra = jnp.einsum('zbhts,zbhsv->zbhtv', attn, vc)
        k_dec = kc * jnp.exp(b_last - bcum)
        new_state = (jnp.exp(b_last[..., 0, :])[..., None] * state
                     + jnp.einsum('zbhsk,zbhsv->zbhkv', k_dec, vc))
        return new_state, o_inter + o_intra

    state0 = jnp.zeros((2, B, H, dk, dv), f32)
    _, ys = lax.scan(step, state0, xs)
    ys = ys.transpose(1, 2, 0, 4, 3, 5).reshape(2, B, S, H, dv)
    o = ys[0] + ys[1][:, ::-1]
    o = rmsnorm(o, norm_g) * jax.nn.silu(g.reshape(B, S, H, dv).astype(f32))
    return o.reshape(B, S, H * dv).astype(q.dtype)


def conv_glu(h, w_gate, w_up, conv_w, conv_b, w_down):
    S = h.shape[1]
    a = h @ w_gate
    u = h @ w_up
    pad = CONV_WIDTH // 2
    ap = jnp.pad(a, ((0, 0), (pad, pad), (0, 0)))
    c = conv_b
    for j in range(CONV_WIDTH):
        c = c + ap[:, j:j + S] * conv_w[j]
    return (jax.nn.gelu(c, approximate=False) * u) @ w_down


def setup_inputs(seed: int = 0) -> dict:
    key = jax.random.key(seed)
    ks = jax.random.split(key, 24)
    f32 = jnp.float32

    def nrm(k, shape, scale):
        return jax.random.normal(k, shape, f32) * scale

    def gain(k, shape):
        return 1.0 + 0.05 * jax.random.normal(k, shape, f32)

    x = nrm(ks[0], (BATCH, SEQ, D_MODEL), 1.0)
    p = nrm(ks[1], (DEPTH, BATCH, SEQ, PLE_DIM), 1.0)
    offsets = jax.random.randint(ks[2], (BATCH, 1), 0, 1024, dtype=jnp.int32)
    positions = offsets + jnp.arange(SEQ, dtype=jnp.int32)[None, :]
    return {
        "x": x,
        "p": p,
        "positions": positions,
        "norm_mix_g": gain(ks[3], (DEPTH, D_MODEL)),
        "w_in": nrm(ks[4], (DEPTH, D_MODEL, IN_COLS), D_MODEL ** -0.5),
        "lam_q1": nrm(ks[5], (DEPTH, DA_HEAD_DIM), 0.1),
        "lam_k1": nrm(ks[6], (DEPTH, DA_HEAD_DIM), 0.1),
        "lam_q2": nrm(ks[7], (DEPTH, DA_HEAD_DIM), 0.1),
        "lam_k2": nrm(ks[8], (DEPTH, DA_HEAD_DIM), 0.1),
        "da_subln_g": gain(ks[9], (DEPTH, DA_V_DIM)),
        "hg_lb_gamma": gain(ks[10], (2, DEPTH + 1, HG_HEADS * HG_KEY_DIM)),
        "hg_norm_g": gain(ks[11], (DEPTH, HG_VAL_DIM)),
        "w_out": nrm(ks[12], (DEPTH, MIX_WIDTH, D_MODEL), MIX_WIDTH ** -0.5),
        "norm_ffn_g": gain(ks[13], (DEPTH, D_MODEL)),
        "w_ffn_gate": nrm(ks[14], (DEPTH, D_MODEL, D_FF), D_MODEL ** -0.5),
        "w_ffn_up": nrm(ks[15], (DEPTH, D_MODEL, D_FF), D_MODEL ** -0.5),
        "ffn_conv_w": nrm(ks[16], (DEPTH, CONV_WIDTH, D_FF), CONV_WIDTH ** -0.5),
        "ffn_conv_b": nrm(ks[17], (DEPTH, D_FF), 0.02),
        "w_ffn_down": nrm(ks[18], (DEPTH, D_FF, D_MODEL), D_FF ** -0.5),
        "norm_ple_g": gain(ks[19], (DEPTH, D_MODEL)),
        "w_ple": nrm(ks[20], (DEPTH, PLE_DIM, D_MODEL), PLE_DIM ** -0.5),
        "w_ple_gate": nrm(ks[21], (DEPTH, D_MODEL, D_MODEL), D_MODEL ** -0.5),
        "final_norm_g": gain(ks[22], (D_MODEL,)),
    }


def reference(x, p, positions, norm_mix_g, w_in, lam_q1, lam_k1, lam_q2, lam_k2,
              da_subln_g, hg_lb_gamma, hg_norm_g, w_out, norm_ffn_g, w_ffn_gate,
              w_ffn_up, ffn_conv_w, ffn_conv_b, w_ffn_down, norm_ple_g, w_ple,
              w_ple_gate, final_norm_g):
    B, S = x.shape[0], x.shape[1]
    for i in range(DEPTH):
        h = rmsnorm(x, norm_mix_g[i])
        proj = h @ w_in[i]
        da_q, da_k, da_v, hg_q, hg_zf, hg_zb, hg_v, hg_g = jnp.split(proj, IN_SPLIT_POINTS, axis=-1)
        da_q = partial_rotary(da_q.reshape(B, S, DA_HEADS, 2, DA_HEAD_DIM), positions)
        da_k = partial_rotary(da_k.reshape(B, S, DA_HEADS, 2, DA_HEAD_DIM), positions)
        da_v = da_v.reshape(B, S, DA_HEADS, DA_V_DIM)
        lambda_init = 0.8 - 0.6 * math.exp(-0.3 * i)
        o_da = diff_attention(da_q, da_k, da_v, lam_q1[i], lam_k1[i], lam_q2[i], lam_k2[i],
                              da_subln_g[i], lambda_init)
        o_hg = hgrn2_bidirectional(hg_q, hg_zf, hg_zb, hg_v, hg_g, hg_lb_gamma, hg_norm_g[i], i)
        x = x + jnp.concatenate([o_da, o_hg], axis=-1) @ w_out[i]
        x = x + conv_glu(rmsnorm(x, norm_ffn_g[i]), w_ffn_gate[i], w_ffn_up[i],
                         ffn_conv_w[i], ffn_conv_b[i], w_ffn_down[i])
        gate = jax.nn.sigmoid(rmsnorm(x, norm_ple_g[i]) @ w_ple_gate[i])
        x = x + (p[i] @ w_ple[i]) * gate
    return rmsnorm(x, final_norm_g)
```

```python
import contextlib
import math
import numpy as np
import concourse.bass as bass
import concourse.mybir as mybir
from concourse.bass_utils import run_bass_kernel_spmd

F32 = mybir.dt.float32
BF16 = mybir.dt.bfloat16
I32 = mybir.dt.int32
U8 = mybir.dt.uint8
AF = mybir.ActivationFunctionType
ALU = mybir.AluOpType
AX = mybir.AxisListType
DTS = {F32: 4, BF16: 2, I32: 4, U8: 1}

ENGS = ("pe", "act", "dve", "pool", "sp")
D = 1024
DFF = 2816
NFC = DFF // 128
EPS = 1e-6
ARENA_BYTES = 207 * 1024


class Op:
    __slots__ = ("eng", "fn", "raw", "war", "dma", "sig", "idx", "dval", "need")

    def __init__(self, eng, fn, dma):
        self.eng = eng
        self.fn = fn
        self.raw = set()
        self.war = set()
        self.dma = dma
        self.sig = 0
        self.dval = 0
        self.need = False


class Prog:
    def __init__(self, nc):
        self.nc = nc
        self.ops = []
        self.q = {e: [] for e in ENGS}
        self.lastw = {}
        self.readers = {}
        self.dma_cnt = {}
        self.last_dma = {}
        self.last_real = {}
        self.lastx = {}
        self.phys = {}
        self.phys_cnt = []
        self.stack = contextlib.ExitStack()
        self.ntile = 0

    def sbuf(self, shape, dtype, name=None):
        self.ntile += 1
        return self.stack.enter_context(
            self.nc.sbuf_tensor(name or f"sb{self.ntile}", list(shape), dtype))

    def psum(self, shape, dtype, name=None):
        self.ntile += 1
        return self.stack.enter_context(
            self.nc.psum_tensor(name or f"ps{self.ntile}", list(shape), dtype))

    def op(self, eng, fn, reads=(), writes=(), dma=None, excl=()):
        o = Op(eng, fn, dma)
        o.idx = len(self.ops)
        for x in excl:
            la = self.lastx.get(x)
            if la is not None and (la.eng != eng or la.dma is not None or dma is not None):
                o.raw.add(la)
            self.lastx[x] = o
        for r in reads:
            w = self.lastw.get(r)
            if w is not None:
                o.raw.add(w)
        for w in writes:
            lw = self.lastw.get(w)
            if lw is not None:
                o.raw.add(lw)
            for rd in self.readers.get(w, ()):
                if rd is not o:
                    o.war.add(rd)
        for w in writes:
            self.lastw[w] = o
            self.readers[w] = []
        for r in reads:
            if r not in writes:
                self.readers.setdefault(r, []).append(o)
        if dma is not None:
            pi = self.phys.get(dma)
            if pi is None:
                pi = len(self.phys)
                self.phys[dma] = pi
                if pi >= len(self.phys_cnt):
                    self.phys_cnt.append(0)
            self.phys_cnt[pi] += 16
            o.dma = pi
            o.dval = self.phys_cnt[pi]
            self.last_dma[pi] = o
        self.ops.append(o)
        self.q[eng].append(o)
        if fn is not None:
            self.last_real[eng] = o
        return o

    def barrier(self):
        deps = list(self.last_real.values()) + list(self.last_dma.values())
        for e in ENGS:
            o = Op(e, None, None)
            o.idx = len(self.ops)
            o.raw.update(deps)
            self.ops.append(o)
            self.q[e].append(o)
        self.lastw.clear()
        self.readers.clear()
        self.lastx.clear()
        self.phys.clear()
        self.last_dma.clear()

    def _edges(self, o):
        for p in o.raw:
            if p.dma is None and p.eng == o.eng and o.eng == "pe" and o.dma is None and o.fn is not None:
                continue
            yield p
        for p in o.war:
            if p.dma is None and p.eng == o.eng and o.dma is None:
                continue
            yield p

    def emit(self):
        nc = self.nc
        for o in self.ops:
            for p in self._edges(o):
                p.need = True
        cnt = {e: 0 for e in ENGS}
        for o in self.ops:
            if o.dma is None and o.need:
                cnt[o.eng] += 1
                o.sig = cnt[o.eng]
        st = self.stack
        esem = {e: st.enter_context(nc.semaphore(f"s_{e}")) for e in ENGS}
        dsem = {i: st.enter_context(nc.semaphore(f"d_{i}")) for i in range(len(self.phys_cnt))}
        self.nsig = cnt
        prog = self

        def run(ename, eng):
            waited = {}
            for o in prog.q[ename]:
                for p in prog._edges(o):
                    if p.dma is not None:
                        s, v = dsem[p.dma], p.dval
                    else:
                        s, v = esem[p.eng], p.sig
                    key = id(s)
                    if waited.get(key, 0) >= v:
                        continue
                    waited[key] = v
                    eng.wait_ge(s, v)
                if o.fn is None:
                    continue
                ins = o.fn(eng)
                if o.dma is not None:
                    ins.then_inc(dsem[o.dma], 16)
                elif o.need:
                    ins.then_inc(esem[ename], 1)

        with nc.Block() as block:
            @block.tensor
            def _(e):
                run("pe", e)

            @block.scalar
            def _(e):
                run("act", e)

            @block.vector
            def _(e):
                run("dve", e)

            @block.gpsimd
            def _(e):
                run("pool", e)

            @block.sync
            def _(e):
                run("sp", e)
        st.close()


class Arena:
    def __init__(self, ap, size):
        self.ap = ap
        self.size = size
        self.off = 0

    def alloc(self, shape, dt, parts=128):
        n = 1
        for s in shape[1:]:
            n *= s
        nb = n * DTS[dt]
        nb_al = (nb + 31) // 32 * 32
        assert self.off + nb_al <= self.size, f"arena overflow {self.off}+{nb_al}>{self.size}"
        v = self.ap[0:shape[0], self.off:self.off + nb].bitcast(dt)
        self.off += nb_al
        if len(shape) == 3:
            v = v.rearrange("p (a b) -> p a b", a=shape[1])
        elif len(shape) == 4:
            v = v.rearrange("p (a b c) -> p a b c", a=shape[1], b=shape[2])
        return v

    def mark(self):
        return self.off

    def release(self, m):
        self.off = m


def skew(stages, n):
    K = len(stages)
    for t in range(n + K - 1):
        for k in range(K):
            i = t - k
            if 0 <= i < n:
                stages[k](i)


class Ring:
    def __init__(self, name, tiles):
        self.name = name
        self.tiles = tiles
        self.i = 0

    def next(self):
        j = self.i % len(self.tiles)
        self.i += 1
        return self.tiles[j], (self.name, j)


class KB:
    def __init__(self, S, dbg=False, phases=(1, 2, 3, 4, 5)):
        self.S = S
        self.NT = S // 128
        self.NCH = S // 64
        self.dbg = dbg
        self.phases = phases
        nc = bass.Bass("TRN2", target_bir_lowering=False)
        self.nc = nc
        self.P = Prog(nc)
        self.arena = Arena(self.P.sbuf([128, ARENA_BYTES], U8, "arena")[:, :], ARENA_BYTES)
        self.psall = self.P.psum([128, 4096], F32, "psall")
        self.inputs()
        self.scratch()

    def din(self, name, shape, dt=F32):
        return self.nc.dram_tensor(name, list(shape), dt, kind="ExternalInput").ap()

    def inputs(self):
        S, NT = self.S, self.NT
        self.x = self.din("x", [S, D])
        self.p_in = self.din("p", [S, 256])
        self.pos = self.din("pos", [128, NT], I32)
        self.w_in = self.din("w_in", [D, 4096])
        self.w_out = self.din("w_out", [D, D])
        self.w_gate = self.din("w_gate", [D, DFF])
        self.w_up = self.din("w_up", [D, DFF])
        self.w_down = self.din("w_down", [DFF, D])
        self.w_ple = self.din("w_ple", [256, D])
        self.w_pleg = self.din("w_pleg", [D, D])
        self.g_pk = self.din("g_pk", [128, 3, 8])
        self.gam = self.din("gam", [128, 2, 2, 512])
        self.lamv = self.din("lamv", [128, 4, 64])
        self.gsub = self.din("gsub", [128, 128])
        self.ghg = self.din("ghg", [128, 128])
        self.gfin = self.din("gfin", [128, D])
        self.convp = self.din("convp", [128, 4, NFC])
        self.c_ident = self.din("c_ident", [128, 128])
        self.c_tri = self.din("c_tri", [128, 4, 128])
        self.c_ind = self.din("c_ind", [128, 2])
        self.c_mask = self.din("c_mask", [64, 2, 64])
        self.c_invf = self.din("c_invf", [128, 8])
        self.out = self.nc.dram_tensor("out", [S, D], F32, kind="ExternalOutput").ap()

    def dscr(self, name, shape, dt):
        kind = "ExternalOutput" if self.dbg else "Internal"
        return self.nc.dram_tensor(name, list(shape), dt, kind=kind).ap()

    def scratch(self):
        S = self.S
        self.qT = self.dscr("s_qT", [512, S], BF16)
        self.kT = self.dscr("s_kT", [512, S], BF16)
        self.vS = self.dscr("s_v", [S, 512], BF16)
        self.hqT = self.dscr("s_hqT", [2, 512, S], BF16)
        self.hkT = self.dscr("s_hkT", [2, 512, S], BF16)
        self.hkd = self.dscr("s_hkd", [2, S, 512], BF16)
        self.hvS = self.dscr("s_hv", [S, 512], BF16)
        self.gateS = self.dscr("s_gate", [S, 512], F32)
        self.ofS = self.dscr("s_of", [S, 512], F32)
        self.mixS = self.dscr("s_mix", [S, D], BF16)
        self.x1S = self.dscr("s_x1", [S, D], F32)
        self.h2T = self.dscr("s_h2T", [D, S + 2], BF16)
        self.x2S = self.dscr("s_x2", [S, D], F32)

    def bank(self, i, n=1):
        return self.psall[:, i * 512:(i + n) * 512]

    def bank_bf(self, i, a=8):
        return self.bank(i).bitcast(BF16).rearrange("p (a b) -> p a b", a=a)

    def mm(self, out, lhsT, rhs, start=True, stop=True, R=(), W=(), X=(), **kw):
        self.P.op("pe", lambda e: e.matmul(out, lhsT=lhsT, rhs=rhs, start=start, stop=stop, **kw), R, W, excl=X)

    def tr(self, out, in_, R=(), W=(), X=(), ident=None):
        ident = self.ident if ident is None else ident
        self.P.op("pe", lambda e: e.transpose(out=out, in_=in_, identity=ident), R, W, excl=X)

    def act(self, out, in_, func, R=(), W=(), scale=1.0, bias=None, accum=None, X=()):
        def fn(e):
            kw = {}
            if bias is not None:
                kw["bias"] = bias
            if accum is not None:
                kw["accum_out"] = accum
            return e.activation(out=out, in_=in_, func=func, scale=scale, **kw)
        self.P.op("act", fn, R, W, excl=X)

    def ts(self, eng, out, in0, s1, s2, op0, op1=None, R=(), W=(), accum=None, X=()):
        def fn(e):
            kw = {}
            if accum is not None:
                kw["accum_out"] = accum
            if op1 is None:
                return e.tensor_scalar(out=out, in0=in0, scalar1=s1, scalar2=None, op0=op0, **kw)
            return e.tensor_scalar(out=out, in0=in0, scalar1=s1, scalar2=s2, op0=op0, op1=op1, **kw)
        self.P.op(eng, fn, R, W, excl=X)

    def tt(self, eng, out, in0, in1, op, R=(), W=(), X=()):
        self.P.op(eng, lambda e: e.tensor_tensor(out=out, in0=in0, in1=in1, op=op), R, W, excl=X)

    def stt(self, out, in0, scalar, in1, op0, op1, R=(), W=(), accum=None, X=()):
        def fn(e):
            kw = {}
            if accum is not None:
                kw["accum_out"] = accum
            return e.scalar_tensor_tensor(out=out, in0=in0, scalar=scalar, in1=in1, op0=op0, op1=op1, **kw)
        self.P.op("dve", fn, R, W, excl=X)

    def cp(self, eng, out, in_, R=(), W=(), X=()):
        if eng == "act":
            self.P.op("act", lambda e: e.copy(out=out, in_=in_), R, W, excl=X)
        else:
            self.P.op(eng, lambda e: e.tensor_copy(out=out, in_=in_), R, W, excl=X)

    def recip(self, out, in_, R=(), W=(), X=()):
        self.P.op("dve", lambda e: e.reciprocal(out=out, in_=in_), R, W, excl=X)

    def memset(self, eng, ap, val, R=(), W=()):
        self.P.op(eng, lambda e: e.memset(ap, val), R, W)

    def dma(self, out, in_, R=(), W=(), key=None, eng="sp", slow=False):
        if slow:
            self.P.op(eng, lambda e: e.dma_start(out=out, in_=in_, allow_slow_non_contiguous=True), R, W, dma=key)
        else:
            self.P.op(eng, lambda e: e.dma_start(out=out, in_=in_), R, W, dma=key)

    def rstd_from_ss(self, rstd, ss, n, R, W, tmpkey):
        mh = self.mhalf[0:rstd.shape[0], 0:1]
        if rstd.shape[1] != 1:
            mh = mh.to_broadcast([rstd.shape[0], rstd.shape[1]])
        self.ts("pool", rstd, ss, 1.0 / n, EPS, ALU.mult, ALU.add, R=R, W=[tmpkey])
        self.tt("pool", rstd, rstd, mh, ALU.pow, R=[tmpkey, "mhalf"], W=W)

    def setup(self):
        A = self.arena
        S, NT = self.S, self.NT
        self.identf = A.alloc([128, 128], F32)
        self.ident = A.alloc([128, 128], BF16)
        self.tri = A.alloc([128, 4, 128], F32)
        self.ind = A.alloc([128, 2], F32)
        self.mask = A.alloc([64, 2, 64], F32)
        self.mhalf = A.alloc([128, 1], F32)
        self.gpk = A.alloc([128, 3, 8], F32)
        self.GS = A.alloc([128, 128], F32)
        self.GHG = A.alloc([128, 128], F32)
        self.lamneg = A.alloc([128, 1], F32)
        self.EDEC = A.alloc([128, 2, 4, self.NCH], F32)
        self.zero = A.alloc([128, 16], BF16)
        ld = [(self.identf, self.c_ident), (self.tri, self.c_tri), (self.ind, self.c_ind),
              (self.mask, self.c_mask), (self.gpk, self.g_pk), (self.GS, self.gsub),
              (self.GHG, self.ghg)]
        for i, (dst, src) in enumerate(ld):
            self.dma(dst, src, W=[("c", i)], key=("c", i))
        self.cp("dve", self.ident, self.identf, R=[("c", 0)], W=["ident"])
        self.memset("pool", self.mhalf, -0.5, W=["mhalf"])
        self.memset("pool", self.zero, 0.0, W=["zero"])
        self.ts("dve", self.GS, self.GS, 0.8, None, ALU.mult, R=[("c", 5)], W=[("c", 5)])
        m = A.mark()
        lv = A.alloc([128, 4, 64], F32)
        pr = A.alloc([128, 2, 64], F32)
        sm = A.alloc([128, 2], F32)
        self.dma(lv, self.lamv, W=["lv"], key="lv")
        self.tt("dve", pr, lv[:, 0:4:2, :], lv[:, 1:4:2, :], ALU.mult, R=["lv"], W=["pr"])
        self.P.op("dve", lambda e: e.tensor_reduce(out=sm, in_=pr, axis=AX.X, op=ALU.add), ["pr"], ["sm"])
        self.act(sm, sm, AF.Exp, R=["sm"], W=["sm"])
        self.stt(self.lamneg, sm[:, 1:2], -0.2, sm[:, 0:1], ALU.add, ALU.subtract, R=["sm"], W=["lamneg"])
        A.release(m)

    def phase1(self):
        A = self.arena
        S, NT = self.S, self.NT
        m0 = A.mark()
        WIN = A.alloc([128, 8, 4096], BF16)
        wst = Ring("wst", [A.alloc([128, 1024], F32) for _ in range(2)])
        LB0 = A.alloc([128, 2, 512], F32)
        LB1 = A.alloc([128, 2, 512], F32)
        COS = A.alloc([128, NT, 8], F32)
        SIN = A.alloc([128, NT, 8], F32)
        m1 = A.mark()
        gm = A.alloc([128, 2, 2, 512], F32)
        self.dma(gm, self.gam, W=["gm"], key="gm")
        self.tt("dve", LB0, gm[:, :, 1, :], gm[:, :, 0, :], ALU.subtract, R=["gm"], W=["LB0"])
        self.act(LB0, LB0, AF.Exp, R=["LB0"], W=["LB0"])
        self.ts("dve", LB0, LB0, 1.0, None, ALU.add, R=["LB0"], W=["LB0"])
        self.recip(LB0, LB0, R=["LB0"], W=["LB0"])
        self.ts("dve", LB1, LB0, -1.0, 1.0, ALU.mult, ALU.add, R=["LB0"], W=["LB1"])
        posi = A.alloc([128, NT], I32)
        posf = A.alloc([128, NT], F32)
        invf = A.alloc([128, 8], F32)
        ang = A.alloc([128, NT, 8], F32)
        self.dma(posi, self.pos, W=["posi"], key="posi")
        self.dma(invf, self.c_invf, W=["invf"], key="invf")
        self.cp("dve", posf, posi, R=["posi"], W=["posf"])
        self.tt("dve", ang, posf.unsqueeze(2).to_broadcast([128, NT, 8]),
                invf.unsqueeze(1).to_broadcast([128, NT, 8]), ALU.mult, R=["posf", "invf"], W=["ang"])
        TWO_PI = 2.0 * math.pi
        C1 = 6.28125
        C2 = TWO_PI - C1
        PI_LO = 3.1415925
        ti = A.alloc([128, NT, 8], I32)
        tf = A.alloc([128, NT, 8], F32)
        for dst, dk, add in ((SIN, "SIN", 0.0), (COS, "COS", 0.5 * math.pi)):
            self.ts("dve", tf, ang, 1.0 / TWO_PI, add / TWO_PI, ALU.mult, ALU.add, R=["ang"], W=["tf"])
            self.cp("dve", ti, tf, R=["tf"], W=["ti"])
            self.cp("dve", tf, ti, R=["ti"], W=["tf"])
            self.ts("dve", dst, ang, add, None, ALU.add, R=["ang"], W=[dk])
            self.stt(dst, tf, -C1, dst, ALU.mult, ALU.add, R=["tf", dk], W=[dk])
            self.stt(dst, tf, -C2, dst, ALU.mult, ALU.add, R=["tf", dk], W=[dk])
            self.ts("dve", dst, dst, -PI_LO, PI_LO, ALU.max, ALU.min, R=[dk], W=[dk])
            self.act(dst, dst, AF.Sin, R=[dk], W=[dk])
        A.release(m1)
        i = 0
        for kc in range(8):
            for cb in range(4):
                st, k = wst.next()
                self.dma(st, self.w_in[kc * 128:(kc + 1) * 128, cb * 1024:(cb + 1) * 1024], W=[k], key=k)
                dst = WIN[:, kc, cb * 1024:(cb + 1) * 1024]
                g = self.gpk[:, 0, kc:kc + 1]
                if i % 2 == 0:
                    self.ts("dve", dst, st, g, None, ALU.mult, R=[k, ("c", 4)], W=[("WIN", kc, cb)])
                else:
                    self.ts("pool", dst, st, g, 0.0, ALU.mult, ALU.add, R=[k, ("c", 4)], W=[("WIN", kc, cb)])
                i += 1
        xt = Ring("xt", [A.alloc([128, D], F32) for _ in range(4)])
        jkr = Ring("junk1", [A.alloc([128, D], BF16) for _ in range(1)])
        ssr = Ring("ss", [A.alloc([128, 1], F32) for _ in range(4)])
        rsr = Ring("rs", [A.alloc([128, 1], F32) for _ in range(4)])
        xnr = Ring("xn", [A.alloc([128, D], BF16) for _ in range(2)])
        hTr = Ring("hT", [A.alloc([128, 8, 128], BF16) for _ in range(2)])
        qkr = Ring("qk", [A.alloc([128, 16, 64], BF16) for _ in range(2)])
        rt = Ring("rt", [A.alloc([128, 16, 8], F32) for _ in range(4)])
        qkTs = Ring("qkTs", [A.alloc([128, 8, 256], BF16) for _ in range(2)])
        vsr = Ring("vs", [A.alloc([128, 512], BF16) for _ in range(2)])
        hvr = Ring("hvs", [A.alloc([128, 512], BF16) for _ in range(2)])
        w32 = Ring("w32", [A.alloc([128, 512], F32) for _ in range(6)])
        lfr = Ring("lfr", [A.alloc([128, 512], F32) for _ in range(4)])
        kinr = Ring("kinr", [A.alloc([128, 512], F32) for _ in range(4)])
        hqr = Ring("hq", [A.alloc([128, 512], F32) for _ in range(2)])
        sgr = Ring("sg", [A.alloc([128, 512], F32) for _ in range(2)])
        b16 = Ring("b16", [A.alloc([128, 512], BF16) for _ in range(12)])
        kdr = Ring("kd", [A.alloc([128, 512], BF16) for _ in range(4)])
        hTs = [Ring(f"hTs{d}", [A.alloc([128, 8, 256], BF16) for _ in range(2)]) for d in range(2)]
        tpr = Ring("tpb", [(self.bank_bf(0), 0), (self.bank_bf(7), 7)])
        pqk = self.bank(1, 2)
        XQK = [("B", 1), ("B", 2)]
        pzr = Ring("pz", [(self.bank(3), 3), (self.bank(4), 4)])
        ptA = self.bank(5)
        ptB = self.bank(6)
        XA = [("B", 5)]
        XB = [("B", 6)]
        xn_cur = {}

        xload = {}

        def load_x(n):
            x_t, xk = xt.next()
            self.dma(x_t, self.x[n * 128:(n + 1) * 128, :], W=[xk], key=xk)
            xload[n] = (x_t, xk)

        rms_cur = {}

        def rmsA(n):
            if n == 0:
                load_x(0)
            if n + 1 < NT:
                load_x(n + 1)
            x_t, xk = xload.pop(n)
            ss, sk = ssr.next()
            rs, rk = rsr.next()
            junk, jk = jkr.next()
            self.act(junk, x_t, AF.Square, R=[xk], W=[sk, jk], accum=ss)
            self.rstd_from_ss(rs, ss, D, R=[sk], W=[rk], tmpkey=("rstmp", rk))
            rms_cur[n] = (x_t, xk, rs, rk)

        def rmsB(n):
            x_t, xk, rs, rk = rms_cur.pop(n)
            xn, nk = xnr.next()
            self.act(xn, x_t, AF.Copy, R=[xk, rk], W=[nk], scale=rs[:, 0:1])
            xn_cur[n] = (xn, nk)

        def proj(out, cg, hT, hk, W, X):
            for kc in range(8):
                self.mm(out, hT[:, kc, :], WIN[:, kc, cg * 512:(cg + 1) * 512], start=(kc == 0),
                        stop=(kc == 7), R=[hk, ("WIN", kc, cg // 2)], W=W, X=X)

        stage_state = {}

        def tstage(ring, key, n, srcs, dst_fn, eng):
            j = n % 2
            if j == 0:
                stage_state[key] = ring.next()
            stg, sk = stage_state[key]
            (pt, bk), pk = tpr.next()
            X = [("B", bk)]
            for i, (src, srck) in enumerate(srcs):
                self.tr(pt[:, i, :], src, R=[srck, "ident"], W=[pk], X=X)
            self.cp(eng, stg[:, :, j * 128:(j + 1) * 128], pt, R=[pk], W=[sk], X=X)
            if j == 1 or n == NT - 1:
                dst_fn(stg, sk, n - j, (j + 1) * 128)

        carry = {}
        carryB = {}

        def mainA(n):
            xn, nk = xn_cur.pop(n)
            (tp, bk), tpk = tpr.next()
            XT = [("B", bk)]
            for kc in range(8):
                self.tr(tp[:, kc, :], xn[:, kc * 128:(kc + 1) * 128], R=[nk, "ident"], W=[tpk], X=XT)
            hT, hk = hTr.next()
            self.cp("act", hT, tp, R=[tpk], W=[hk], X=XT)
            kin = {}
            lfs = {}
            for d in range(2):
                (pz, bz), pk = pzr.next()
                XZ = [("B", bz)]
                proj(pz, 4 + d, hT, hk, [pk], XZ)
                E, ek = kinr.next()
                self.act(E, pz, AF.Exp, R=[pk], W=[ek], scale=-1.0, X=XZ)
                self.ts("pool", E, E, 1.0, 1.0, ALU.mult, ALU.add, R=[ek], W=[ek])
                kin[d] = (E, ek)
            for d in range(2):
                E, ek = kin[d]
                self.recip(E, E, R=[ek], W=[ek])
                self.tt("dve", E, E, LB1[:, d, :], ALU.mult, R=[ek, "LB1"], W=[ek])
                self.tt("pool", E, E, LB0[:, d, :], ALU.add, R=[ek, "LB0"], W=[ek])
            (pz, bz), pk = pzr.next()
            XZ = [("B", bz)]
            proj(pz, 3, hT, hk, [pk], XZ)
            HQ, hqk = hqr.next()
            self.cp("act", HQ, pz, R=[pk], W=[hqk], X=XZ)
            proj(pqk[:, 0:512], 0, hT, hk, ["pqk0"], [("B", 1)])
            proj(pqk[:, 512:1024], 1, hT, hk, ["pqk1"], [("B", 2)])
            qk, qkk = qkr.next()
            pq3 = pqk.rearrange("p (a b) -> p a b", a=16)
            self.cp("act", qk, pq3, R=["pqk0", "pqk1"], W=[qkk], X=XQK)
            cosb = COS[:, n:n + 1, :].to_broadcast([128, 16, 8])
            sinb = SIN[:, n:n + 1, :].to_broadcast([128, 16, 8])
            x1 = pq3[:, :, 0:8]
            x2 = pq3[:, :, 8:16]
            t1, k1 = rt.next()
            t2, k2 = rt.next()
            t3, k3 = rt.next()
            t4, k4 = rt.next()
            self.tt("dve", t1, x1, cosb, ALU.mult, R=["pqk0", "pqk1", "COS"], W=[k1], X=XQK)
            self.tt("dve", t2, x2, sinb, ALU.mult, R=["pqk0", "pqk1", "SIN"], W=[k2], X=XQK)
            self.tt("dve", t3, x2, cosb, ALU.mult, R=["pqk0", "pqk1", "COS"], W=[k3], X=XQK)
            self.tt("dve", t4, x1, sinb, ALU.mult, R=["pqk0", "pqk1", "SIN"], W=[k4], X=XQK)
            self.tt("pool", qk[:, :, 0:8], t1, t2, ALU.subtract, R=[k1, k2, qkk], W=[qkk])
            self.tt("pool", qk[:, :, 8:16], t3, t4, ALU.add, R=[k3, k4, qkk], W=[qkk])
            (pz, bz), pk = pzr.next()
            XZ = [("B", bz)]
            proj(pz, 2, hT, hk, [pk], XZ)
            vs, vk = vsr.next()
            self.cp("act", vs, pz, R=[pk], W=[vk], X=XZ)
            self.dma(self.vS[n * 128:(n + 1) * 128, :], vs, R=[vk], W=[("vS", n)], key=vk)
            (pz, bz), pk = pzr.next()
            XZ = [("B", bz)]
            proj(pz, 6, hT, hk, [pk], XZ)
            hv, hvk = hvr.next()
            self.cp("act", hv, pz, R=[pk], W=[hvk], X=XZ)
            self.dma(self.hvS[n * 128:(n + 1) * 128, :], hv, R=[hvk], W=[("hvS", n)], key=hvk)
            (pz, bz), pk = pzr.next()
            XZ = [("B", bz)]
            proj(pz, 7, hT, hk, [pk], XZ)
            E, ek = w32.next()
            self.act(E, pz, AF.Exp, R=[pk], W=[ek], scale=-1.0, X=XZ)
            self.ts("pool", E, E, 1.0, 1.0, ALU.mult, ALU.add, R=[ek], W=[ek])
            self.recip(E, E, R=[ek], W=[ek])
            SG, sgk = sgr.next()
            self.tt("dve", SG, pz, E, ALU.mult, R=[pk, ek], W=[sgk], X=XZ)
            self.dma(self.gateS[n * 128:(n + 1) * 128, :], SG, R=[sgk], W=[("gateS", n)], key=sgk)
            for d in range(2):
                E, ek = kin[d]
                LF, lk = lfr.next()
                self.act(LF, E, AF.Ln, R=[ek], W=[lk])
                self.ts("pool", E, E, -1.0, 1.0, ALU.mult, ALU.add, R=[ek, lk], W=[ek])
                lfs[d] = (LF, lk)
            carry[n] = (qk, qkk, lfs, kin, HQ, hqk)

        def mainB1(n):
            qk, qkk, lfs, kin, HQ, hqk = carry.pop(n)
            pbts = []
            for d in range(2):
                LF, lk = lfs[d]
                pbt = ptA[:, 8 * d:8 * d + 8].rearrange("p (a b) -> p a b", a=4)
                for h in range(4):
                    self.mm(pbt[:, h, :], LF[:, h * 128:(h + 1) * 128], self.ind, R=[lk, ("c", 2)], W=[("pbt", d)], X=XA)
                pbts.append(pbt)
            for d in range(2):
                self.act(self.EDEC[:, d, :, 2 * n:2 * n + 2], pbts[d], AF.Exp, R=[("pbt", d)], W=[("EDEC", d, n)], X=XA)
            qk2 = qk.rearrange("p a b -> p (a b)")

            def st_qk(stg, sk, n0, w):
                self.dma(self.qT.rearrange("(h p) t -> p h t", p=128)[:, :, n0 * 128:n0 * 128 + w],
                         stg[:, 0:4, 0:w], R=[sk], W=[("qT", n0)], key=(sk, "q"))
                self.dma(self.kT.rearrange("(h p) t -> p h t", p=128)[:, :, n0 * 128:n0 * 128 + w],
                         stg[:, 4:8, 0:w], R=[sk], W=[("kT", n0)], key=(sk, "k"))
            tstage(qkTs, "qk", n, [(qk2[:, i * 128:(i + 1) * 128], qkk) for i in range(8)], st_qk, "act")
            prods = {}
            for d in range(2):
                LF, lk = lfs[d]
                KIN, kk = kin[d]
                if d == 0:
                    pA, XAd, kA, pB, XBd, kB = ptA, XA, "ptA", ptB, XB, "ptB"
                else:
                    (pA, b1), kA = pzr.next()
                    (pB, b2), kB = pzr.next()
                    XAd, XBd = [("B", b1)], [("B", b2)]
                self.mm(pA, self.tri[:, 2 * d, :], LF, R=[lk, ("c", 1)], W=[kA], X=XAd)
                self.mm(pB, self.tri[:, 2 * d + 1, :], LF, R=[lk, ("c", 1)], W=[kB], X=XBd)
                X1, xk1 = w32.next()
                self.act(X1, pA, AF.Exp, R=[kA], W=[xk1], X=XAd)
                QT, qtk = b16.next()
                self.tt("dve", QT, HQ, X1, ALU.mult, R=[hqk, xk1], W=[qtk])
                X2, xk2 = w32.next()
                self.act(X2, pA, AF.Exp, R=[kA], W=[xk2], scale=-1.0, X=XAd)
                KT, ktk = b16.next()
                self.tt("dve", KT, KIN, X2, ALU.mult, R=[kk, xk2], W=[ktk])
                X3, xk3 = w32.next()
                self.act(X3, pB, AF.Exp, R=[kB], W=[xk3], X=XBd)
                KD, kdk = kdr.next()
                self.tt("pool", KD, KIN, X3, ALU.mult, R=[kk, xk3], W=[kdk])
                self.dma(self.hkd[d, n * 128:(n + 1) * 128, :], KD, R=[kdk], W=[("hkd", d, n)], key=kdk)
                prods[d] = (QT, qtk, KT, ktk)
            carryB[n] = prods

        def mainB2(n):
            prods = carryB.pop(n)
            for d in range(2):
                QT, qtk, KT, ktk = prods[d]

                def st_h(stg, sk, n0, w, d=d):
                    self.dma(self.hqT[d].rearrange("(h p) t -> p h t", p=128)[:, :, n0 * 128:n0 * 128 + w],
                             stg[:, 0:4, 0:w], R=[sk], W=[("hqT", d, n0)], key=(sk, "q"))
                    self.dma(self.hkT[d].rearrange("(h p) t -> p h t", p=128)[:, :, n0 * 128:n0 * 128 + w],
                             stg[:, 4:8, 0:w], R=[sk], W=[("hkT", d, n0)], key=(sk, "k"))
                srcs = [(QT[:, i * 128:(i + 1) * 128], qtk) for i in range(4)] + \
                       [(KT[:, i * 128:(i + 1) * 128], ktk) for i in range(4)]
                tstage(hTs[d], ("h", d), n, srcs, st_h, "act" if d == 0 else "dve")

        skew([rmsA, rmsB, mainA, mainB1, mainB2], NT)
        self.P.barrier()
        A.release(m0)

    def phase2a(self):
        A = self.arena
        S, NT = self.S, self.NT
        QB = 512
        NQB = S // QB
        NQI = QB // 128
        m0 = A.mark()
        kThr = [A.alloc([128, S], BF16) for _ in range(2)]
        qThr = [A.alloc([128, S], BF16) for _ in range(2)]
        var = [A.alloc([128, NT, 128], BF16) for _ in range(2)]
        ones = A.alloc([128, 32], BF16)
        ptr = Ring("PT", [A.alloc([128, 2, QB], BF16) for _ in range(4)])
        otr = Ring("OTs", [A.alloc([128, 2, QB], F32) for _ in range(2)])
        rwr = Ring("RSs", [A.alloc([128, QB], F32) for _ in range(2)])
        rcr = Ring("rc", [A.alloc([128, NQI, 2], F32) for _ in range(2)])
        rtr = Ring("rt2", [A.alloc([128, NQI, 2], F32) for _ in range(2)])
        nlr = Ring("nl", [A.alloc([128, NQI], F32) for _ in range(2)])
        o0r = Ring("o0", [A.alloc([128, 128], F32) for _ in range(4)])
        oor = Ring("oo", [A.alloc([128, 128], F32) for _ in range(4)])
        ssr = Ring("ss2", [A.alloc([128, 1], F32) for _ in range(4)])
        rsr = Ring("rs2", [A.alloc([128, 1], F32) for _ in range(4)])
        mxr = Ring("mx", [A.alloc([128, NQI, 128], BF16) for _ in range(2)])
        jkr = Ring("junk2", [A.alloc([128, 128], BF16) for _ in range(3)])
        self.memset("pool", ones, 1.0, W=["ones2"])
        vS3 = self.vS.rearrange("(n p) c -> p n c", p=128)
        mix3 = self.mixS.rearrange("(n p) c -> p n c", p=128)
        NG = (NT + 7) // 8
        OTP = self.psall[:, 4 * 512:6 * 512].rearrange("p (m c) -> p m c", m=2)
        TPr = self.bank(7).rearrange("p (a b) -> p a b", a=NQI)
        TPo = self.bank(7).rearrange("p (a b) -> p a b", a=4)
        X7 = [("B", 7)]
        offs = (2, 6, 16, 20) if NT >= 32 else (1, 2, 4, 5)
        pend = []
        gstep = [0]

        def defer(dt, fn):
            pend.append((gstep[0] + dt, fn))

        def run_pending(force=False):
            while pend and (force or pend[0][0] <= gstep[0]):
                pend.pop(0)[1]()

        def load_head(h):
            j = h % 2
            self.dma(kThr[j], self.kT[h * 128:(h + 1) * 128, :], W=[("kTh", j)], key=("kTh", j))
            self.dma(qThr[j], self.qT[h * 128:(h + 1) * 128, :], W=[("qTh", j)], key=("qTh", j))
            for g in range(NG):
                n0, n1 = g * 8, min(NT, g * 8 + 8)
                self.dma(var[j][:, n0:n1, :], vS3[:, n0:n1, h * 128:(h + 1) * 128],
                         W=[("va", j, g)], key=("va", j, g))

        def epilogue(h, qb):
            OT, otk = otr.next()
            self.cp("dve", OT, OTP, R=[("OT", 0), ("OT", 1)], W=[otk], X=[("B", 4), ("B", 5)])
            RW, rwk = rwr.next()
            self.cp("dve", RW, self.bank(6)[:, 0:QB], R=[("RS", g) for g in range(4)], W=[rwk], X=[("B", 6)])
            rc, rck = rcr.next()
            nl, nlk = nlr.next()
            MX, mxk = mxr.next()

            def e2():
                for qi in range(NQI):
                    self.tr(TPr[:, qi, :], RW[:, qi * 128:(qi + 1) * 128], R=[rwk, ("c", 0)], W=["TP7"],
                            X=X7, ident=self.identf)
                rt2, rtk = rtr.next()
                self.cp("dve", rt2, TPr[:, :, 0:64:32], R=["TP7"], W=[rtk], X=X7)
                self.tt("dve", rt2, rt2, TPr[:, :, 64:128:32], ALU.add, R=["TP7", rtk], W=[rtk], X=X7)
                self.recip(rc, rt2, R=[rtk], W=[rck])
                self.ts("dve", nl, rc[:, :, 1], self.lamneg[:, 0:1], None, ALU.mult, R=[rck, "lamneg"], W=[nlk])

            def e34(half):
                def fn():
                    for q2 in range(2):
                        qi = 2 * half + q2
                        for m in range(2):
                            self.tr(TPo[:, 2 * q2 + m, :], OT[:, m, qi * 128:(qi + 1) * 128], R=[otk, ("c", 0)],
                                    W=["TP7"], X=X7, ident=self.identf)
                    for q2 in range(2):
                        qi = 2 * half + q2
                        O0, o0k = o0r.next()
                        self.ts("dve", O0, TPo[:, 2 * q2, :], rc[:, qi, 0:1], None, ALU.mult, R=["TP7", rck], W=[o0k], X=X7)
                        OO, ook = oor.next()
                        self.stt(OO, TPo[:, 2 * q2 + 1, :], nl[:, qi:qi + 1], O0, ALU.mult, ALU.add,
                                 R=["TP7", nlk, o0k], W=[ook], X=X7)
                        ss, ssk = ssr.next()
                        rs, rsk = rsr.next()
                        junk, jk = jkr.next()
                        self.stt(junk, OO, 1.0, OO, ALU.mult, ALU.mult, R=[ook], W=[ssk, jk], accum=ss)
                        self.rstd_from_ss(rs, ss, 128, R=[ssk], W=[rsk], tmpkey=("rstmp2", rsk))
                        self.stt(MX[:, qi, :], OO, rs[:, 0:1], self.GS, ALU.mult, ALU.mult, R=[ook, rsk, ("c", 5)],
                                 W=[(mxk, qi)])
                return fn

            def e5():
                self.dma(mix3[:, qb * NQI:(qb + 1) * NQI, h * 128:(h + 1) * 128], MX,
                         R=[(mxk, qi) for qi in range(NQI)], W=[("mixA", h, qb)], key=mxk)

            defer(offs[0], e2)
            defer(offs[1], e34(0))
            defer(offs[2], e34(1))
            defer(offs[3], e5)

        def head(h):
            j = h % 2
            kTh, qTh, va = kThr[j], qThr[j], var[j]
            steps = [(qb, kt) for qb in range(NQB) for kt in range(NT)]
            n = len(steps)
            rs_prev = [None]

            def QK(i):
                qb, kt = steps[i]
                s = i % 2
                for m in range(2):
                    self.mm(self.bank(2 * s + m), kTh[64 * m:64 * m + 64, kt * 128:(kt + 1) * 128],
                            qTh[64 * m:64 * m + 64, qb * QB:(qb + 1) * QB],
                            R=[("kTh", j), ("qTh", j)], W=[("ST", s, m)], X=[("B", 2 * s + m)])

            def EXP(i):
                s = i % 2
                PT, pk = ptr.next()
                src = self.psall[:, 2 * s * 512:2 * s * 512 + 1024].rearrange("p (m c) -> p m c", m=2)
                self.act(PT, src, AF.Exp, R=[("ST", s, 0), ("ST", s, 1)], W=[pk], scale=0.125,
                         X=[("B", 2 * s), ("B", 2 * s + 1)])
                return PT, pk

            def AV(i, PT, pk):
                qb, kt = steps[i]
                for m in range(2):
                    self.mm(self.bank(4 + m), va[:, kt, :], PT[:, m, :], start=(kt == 0), stop=(kt == NT - 1),
                            R=[pk, ("va", j, kt // 8)], W=[("OT", m)], X=[("B", 4 + m)])
                if kt % 2 == 0:
                    rs_prev[0] = (PT, pk)
                else:
                    PTp, pkp = rs_prev[0]
                    for g, (PTx, pkx, m) in enumerate(((PTp, pkp, 0), (PTp, pkp, 1), (PT, pk, 0), (PT, pk, 1))):
                        self.mm(self.bank(6)[32 * g:32 * g + 32, 0:QB], ones, PTx[:, m, :], start=(kt == 1),
                                stop=(kt == NT - 1), R=[pkx, "ones2"], W=[("RS", g)], X=[("B", 6)],
                                tile_position=(0, 32 * g), skip_group_check=True)

            QK(0)
            for i in range(n):
                if i + 1 < n:
                    QK(i + 1)
                PT, pk = EXP(i)
                AV(i, PT, pk)
                if steps[i][1] == NT - 1:
                    epilogue(h, steps[i][0])
                run_pending()
                gstep[0] += 1

        load_head(0)
        for h in range(4):
            if h + 1 < 4:
                load_head(h + 1)
            head(h)
        run_pending(force=True)
        self.P.barrier()
        A.release(m0)

    def phase2b(self):
        A = self.arena
        S, NT, NCH = self.S, self.NT, self.NCH
        NB = NCH // 8
        m0 = A.mark()
        ST32 = [A.alloc([128, 128], F32) for _ in range(4)]
        STB = [A.alloc([128, 128], BF16) for _ in range(4)]
        QTt = [Ring(f"QTt{c}", [A.alloc([128, 512], BF16) for _ in range(2)]) for c in range(4)]
        KTt = [Ring(f"KTt{c}", [A.alloc([128, 512], BF16) for _ in range(2)]) for c in range(4)]
        KDt = [Ring(f"KDt{c}", [A.alloc([64, 8, 128], BF16) for _ in range(2)]) for c in range(4)]
        VVt = [Ring(f"VVt{c}", [A.alloc([64, 8, 128], BF16) for _ in range(2)]) for c in range(4)]
        OBt = [Ring(f"OB{c}", [A.alloc([64, 8, 128], F32) for _ in range(2)]) for c in range(4)]
        OFt = [Ring(f"OF{c}", [A.alloc([64, 8, 128], F32) for _ in range(2)]) for c in range(4)]
        GTt = [Ring(f"GT{c}", [A.alloc([64, 8, 128], F32) for _ in range(2)]) for c in range(4)]
        AMt = [Ring(f"AM{c}", [A.alloc([64, 4, 64], BF16) for _ in range(2)]) for c in range(4)]
        SQ = Ring("SQ", [A.alloc([64, 8, 128], F32) for _ in range(2)])
        ssr = Ring("ss3", [A.alloc([64, 8], F32) for _ in range(2)])
        MXH = Ring("MXH", [A.alloc([64, 8, 128], BF16) for _ in range(2)])
        hkd3 = [self.hkd[d].rearrange("(c s) k -> s c k", s=64) for d in range(2)]
        hv3 = self.hvS.rearrange("(c s) k -> s c k", s=64)
        of3 = self.ofS.rearrange("(c s) k -> s c k", s=64)
        gt3 = self.gateS.rearrange("(c s) k -> s c k", s=64)
        mixh3 = self.mixS.rearrange("(c s) f -> s c f", s=64)

        for d in range(2):
            order = list(range(NB)) if d == 0 else list(range(NB - 1, -1, -1))
            cur = {}

            def load(c, b, d=d):
                h = c
                t0 = b * 512
                hs = slice(h * 128, (h + 1) * 128)
                q, qk = QTt[c].next()
                k, kk = KTt[c].next()
                kd, kdk = KDt[c].next()
                vv, vk = VVt[c].next()
                self.dma(q, self.hqT[d, hs, t0:t0 + 512], W=[qk], key=qk)
                self.dma(k, self.hkT[d, hs, t0:t0 + 512], W=[kk], key=kk)
                self.dma(kd, hkd3[d][:, 8 * b:8 * b + 8, hs], W=[kdk], key=kdk)
                self.dma(vv, hv3[:, 8 * b:8 * b + 8, hs], W=[vk], key=vk)
                ob, obk = OBt[c].next()
                ent = dict(q=(q, qk), k=(k, kk), kd=(kd, kdk), vv=(vv, vk), ob=(ob, obk))
                if d == 1:
                    of, ofk = OFt[c].next()
                    gt, gtk = GTt[c].next()
                    self.dma(of, of3[:, 8 * b:8 * b + 8, hs], W=[ofk], key=ofk)
                    self.dma(gt, gt3[:, 8 * b:8 * b + 8, hs], W=[gtk], key=gtk)
                    self.tt("pool", gt, gt, self.GHG[0:64, :].unsqueeze(1).to_broadcast([64, 8, 128]), ALU.mult,
                            R=[gtk], W=[gtk])
                    ent["of"] = (of, ofk)
                    ent["gt"] = (gt, gtk)
                cur[(c, b)] = ent

            for c in range(4):
                self.memset("pool", ST32[c], 0.0, W=[("ST32", c)])
                self.memset("pool", STB[c], 0.0, W=[("STB", c)])
                load(c, order[0])
            for bi, b in enumerate(order):
                if bi + 1 < NB:
                    for c in range(4):
                        load(c, order[bi + 1])
                subs = [(0, 4), (4, 8)] if d == 0 else [(4, 8), (0, 4)]
                for (j0, j1) in subs:
                    js = list(range(j0, j1)) if d == 0 else list(range(j1 - 1, j0 - 1, -1))
                    ams = {}
                    for c in range(4):
                        e = cur[(c, b)]
                        q, qk = e["q"]
                        k, kk = e["k"]
                        XA = [("B", 2 * c)]
                        pa = self.bank(2 * c)[0:64, 0:256].rearrange("p (a b) -> p a b", a=4)
                        for jj in range(4):
                            j = j0 + jj
                            self.mm(pa[:, jj, :], k[:, j * 64:(j + 1) * 64], q[:, j * 64:(j + 1) * 64],
                                    R=[qk, kk], W=[("pa", c)], X=XA)
                        am, amk = AMt[c].next()
                        self.tt("dve", am, pa, self.mask[:, d:d + 1, :].to_broadcast([64, 4, 64]), ALU.mult,
                                R=[("pa", c)], W=[amk], X=XA)
                        ams[c] = (am, amk)
                    for j in js:
                        for c in range(4):
                            e = cur[(c, b)]
                            q, qk = e["q"]
                            kd, kdk = e["kd"]
                            vv, vk = e["vv"]
                            ob, obk = e["ob"]
                            am, amk = ams[c]
                            XA = [("B", 2 * c)]
                            XU = [("B", 2 * c + 1)]
                            po = self.bank(2 * c)[0:64, 256:384]
                            pu = self.bank(2 * c + 1)[:, 0:128]
                            ch = 8 * b + j
                            self.mm(po, q[:, j * 64:(j + 1) * 64], STB[c], start=True, stop=False,
                                    R=[qk, ("STB", c)], W=[("po", c)], X=XA)
                            self.mm(po, am[:, j - j0, :], vv[:, j, :], start=False, stop=True,
                                    R=[amk, vk], W=[("po", c)], X=XA)
                            self.mm(pu, kd[:, j, :], vv[:, j, :], R=[kdk, vk], W=[("pu", c)], X=XU)
                            self.cp("act", ob[:, j, :], po, R=[("po", c)], W=[obk], X=XA)
                            self.stt(ST32[c], ST32[c], self.EDEC[:, d, c, ch:ch + 1], pu, ALU.mult, ALU.add,
                                     R=[("ST32", c), ("pu", c)], W=[("ST32", c)], X=XU)
                            self.cp("act", STB[c], ST32[c], R=[("ST32", c)], W=[("STB", c)])
                for c in range(4):
                    e = cur.pop((c, b))
                    ob, obk = e["ob"]
                    hs = slice(c * 128, (c + 1) * 128)
                    if d == 0:
                        self.dma(of3[:, 8 * b:8 * b + 8, hs], ob, R=[obk], W=[("ofS", c, b)], key=obk)
                    else:
                        of, ofk = e["of"]
                        gt, gtk = e["gt"]
                        self.tt("dve", ob, ob, of, ALU.add, R=[obk, ofk], W=[obk])
                        sq, sqk = SQ.next()
                        ss, ssk = ssr.next()
                        for j in range(8):
                            self.stt(sq[:, j, :], ob[:, j, :], 1.0, ob[:, j, :], ALU.mult, ALU.mult, R=[obk],
                                     W=[(ssk, j), (sqk, j)], accum=ss[:, j:j + 1])
                        self.rstd_from_ss(ss, ss, 128, R=[(ssk, j) for j in range(8)], W=[ssk], tmpkey=("rstmp3", ssk))
                        self.tt("dve", ob, ob, ss.unsqueeze(2).to_broadcast([64, 8, 128]), ALU.mult, R=[obk, ssk], W=[obk])
                        mx, mxk = MXH.next()
                        self.tt("dve", mx, ob, gt, ALU.mult, R=[obk, gtk], W=[mxk])
                        self.dma(mixh3[:, 8 * b:8 * b + 8, 512 + c * 128:512 + (c + 1) * 128], mx, R=[mxk],
                                 W=[("mixH", c, b)], key=mxk)
            self.P.barrier()
        A.release(m0)

    def phase3(self):
        A = self.arena
        S, NT = self.S, self.NT
        self.m_ffn = A.mark()
        self.WG = A.alloc([128, 8, DFF], BF16)
        self.WU = A.alloc([128, 8, DFF], BF16)
        self.WD = A.alloc([128, NFC, D], BF16)
        m3 = A.mark()
        WOUT = A.alloc([128, 8, D], BF16)
        wst = Ring("wst3", [A.alloc([128, 704], F32) for _ in range(3)])
        wl = []
        for kc in range(8):
            for cb in range(2):
                wl.append((self.w_out[kc * 128:(kc + 1) * 128, cb * 512:(cb + 1) * 512],
                           WOUT[:, kc, cb * 512:(cb + 1) * 512], None, ("WOUT", kc)))
        n_wout = len(wl)
        for wi, (wd, WS) in enumerate(((self.w_gate, self.WG), (self.w_up, self.WU))):
            for kc in range(8):
                for cb in range(4):
                    wl.append((wd[kc * 128:(kc + 1) * 128, cb * 704:(cb + 1) * 704],
                               WS[:, kc, cb * 704:(cb + 1) * 704], self.gpk[:, 1, kc:kc + 1], ("WGU", wi, kc, cb)))
        for fc in range(NFC):
            for cb in range(2):
                wl.append((self.w_down[fc * 128:(fc + 1) * 128, cb * 512:(cb + 1) * 512],
                           self.WD[:, fc, cb * 512:(cb + 1) * 512], None, ("WD", fc, cb)))

        def wload(i):
            src, dst, sc, res = wl[i]
            st, k = wst.next()
            w = src.shape[1]
            self.dma(st[:, 0:w], src, W=[k], key=k)
            ce = "pool" if i % 2 == 0 else "dve"
            if sc is None:
                self.cp(ce, dst, st[:, 0:w], R=[k], W=[res])
            else:
                self.ts(ce, dst, st[:, 0:w], sc, 0.0, ALU.mult, ALU.add, R=[k], W=[res])

        mxt = Ring("mxt", [A.alloc([128, D], BF16) for _ in range(2)])
        xtr = Ring("xt3", [A.alloc([128, D], F32) for _ in range(2)])
        mTr = Ring("mT", [A.alloc([128, 8, 128], BF16) for _ in range(2)])
        x1r = Ring("x1", [A.alloc([128, D], F32) for _ in range(2)])
        xnr = Ring("xn3", [A.alloc([128, D], BF16) for _ in range(2)])
        h2s = Ring("h2s", [A.alloc([128, 8, 256], BF16) for _ in range(2)])
        ssr = Ring("ss4", [A.alloc([128, 1], F32) for _ in range(4)])
        rsr = Ring("rs4", [A.alloc([128, 1], F32) for _ in range(4)])
        jkr = Ring("junk3", [A.alloc([128, D], BF16) for _ in range(2)])
        h2T3 = self.h2T.rearrange("(kc p) t -> p kc t", p=128)
        zz = self.zero[:, 0:8].rearrange("p (a b) -> p a b", b=1)
        self.dma(h2T3[:, :, 0:1], zz, W=[("h2T", "z0")], key="z0", slow=True)
        self.dma(h2T3[:, :, S + 1:S + 2], zz, W=[("h2T", "z1")], key="z1", slow=True)
        stg_state = {}
        nwc = [0]
        while nwc[0] < n_wout:
            wload(nwc[0])
            nwc[0] += 1
        st3 = {}

        def g0(n):
            mx, mxk = mxt.next()
            self.dma(mx, self.mixS[n * 128:(n + 1) * 128, :], W=[mxk], key=mxk)
            while nwc[0] < len(wl) and nwc[0] < n_wout + (n + 1) * (len(wl) - n_wout) // max(1, NT) + 1:
                wload(nwc[0])
                nwc[0] += 1
            st3[n] = dict(mx=(mx, mxk))

        def g1(n):
            mx, mxk = st3[n]["mx"]
            XT = [("B", 0)]
            tp = self.bank_bf(0)
            for kc in range(8):
                self.tr(tp[:, kc, :], mx[:, kc * 128:(kc + 1) * 128], R=[mxk], W=["tp3"], X=XT)
            mT, mk = mTr.next()
            self.cp("act", mT, tp, R=["tp3"], W=[mk], X=XT)
            st3[n]["mT"] = (mT, mk)

        def g2(n):
            mT, mk = st3[n]["mT"]
            x_t, xk = xtr.next()
            self.dma(x_t, self.x[n * 128:(n + 1) * 128, :], W=[xk], key=xk)
            s = n % 2
            po = self.bank(1 + 2 * s, 2)
            for hf in range(2):
                for kc in range(8):
                    self.mm(po[:, hf * 512:(hf + 1) * 512], mT[:, kc, :], WOUT[:, kc, hf * 512:(hf + 1) * 512],
                            start=(kc == 0), stop=(kc == 7), R=[mk, ("WOUT", kc)], W=[("po3", s)], X=[("B", 1 + 2 * s + hf)])
            st3[n]["x"] = (x_t, xk)

        def g3(n):
            x_t, xk = st3[n]["x"]
            s = n % 2
            po = self.bank(1 + 2 * s, 2)
            XO = [("B", 1 + 2 * s), ("B", 2 + 2 * s)]
            x1, x1k = x1r.next()
            self.tt("dve", x1, po, x_t, ALU.add, R=[("po3", s), xk], W=[x1k], X=XO)
            self.dma(self.x1S[n * 128:(n + 1) * 128, :], x1, R=[x1k], W=[("x1S", n)], key=x1k)
            ss, ssk = ssr.next()
            rs, rsk = rsr.next()
            junk, jk = jkr.next()
            self.stt(junk, x1, 1.0, x1, ALU.mult, ALU.mult, R=[x1k], W=[ssk, jk], accum=ss)
            self.rstd_from_ss(rs, ss, D, R=[ssk], W=[rsk], tmpkey=("rstmp4", rsk))
            xn, xnk = xnr.next()
            self.act(xn, x1, AF.Copy, R=[x1k, rsk], W=[xnk], scale=rs[:, 0:1])
            st3[n]["xn"] = (xn, xnk)

        def g4(n):
            xn, xnk = st3.pop(n)["xn"]
            XT5 = [("B", 5)]
            tp5 = self.bank_bf(5)
            for kc in range(8):
                self.tr(tp5[:, kc, :], xn[:, kc * 128:(kc + 1) * 128], R=[xnk], W=["tp5"], X=XT5)
            j = n % 2
            if j == 0:
                stg_state["h2"] = h2s.next()
            stg, sk = stg_state["h2"]
            self.cp("dve", stg[:, :, j * 128:(j + 1) * 128], tp5, R=["tp5"], W=[sk], X=XT5)
            if j == 1 or n == NT - 1:
                n0 = n - j
                w = (j + 1) * 128
                self.dma(h2T3[:, :, 1 + n0 * 128:1 + n0 * 128 + w], stg[:, :, 0:w], R=[sk], W=[("h2T", n0)], key=sk)

        skew([g0, g1, g2, g3, g4], NT)
        nw = nwc[0]
        while nw < len(wl):
            wload(nw)
            nw += 1
        self.P.barrier()
        A.release(m3)

    def phase4(self):
        A = self.arena
        S, NT = self.S, self.NT
        NSB = S // 512
        m4 = A.mark()
        CP = A.alloc([128, 4, NFC], F32)
        self.dma(CP, self.convp, W=["CP"], key="CP")
        GU = A.alloc([128, NFC, 512], BF16)
        HTr = Ring("HT", [A.alloc([128, 8, 514], BF16) for _ in range(2)])
        Asr = Ring("Asb", [A.alloc([128, 514], F32) for _ in range(2)])
        Cr = Ring("Cc", [A.alloc([128, 512], F32) for _ in range(2)])
        Gr = Ring("Gg", [A.alloc([128, 512], F32) for _ in range(2)])
        x1r = Ring("x1b", [A.alloc([128, D], F32) for _ in range(2)])
        x2r = Ring("x2b", [A.alloc([128, D], F32) for _ in range(2)])
        h2T3 = self.h2T.rearrange("(kc p) t -> p kc t", p=128)

        def loadHT(sb):
            HT, hk = HTr.next()
            self.dma(HT, h2T3[:, :, sb * 512:sb * 512 + 514], W=[hk], key=hk)
            return HT, hk

        nxt = loadHT(0)
        for sb in range(NSB):
            HT, hk = nxt
            if sb + 1 < NSB:
                nxt = loadHT(sb + 1)
            for fc in range(NFC):
                s = fc % 2
                pa = self.bank(s)
                pu = self.bank(2 + s)
                ph = self.bank(4)[:, 2 * s:2 * s + 2]
                fs = slice(fc * 128, (fc + 1) * 128)
                for kc in range(8):
                    self.mm(pa, self.WG[:, kc, fs], HT[:, kc, 1:513], start=(kc == 0), stop=(kc == 7),
                            R=[hk], W=[("pa4", s)], X=[("B", s)])
                for kc in range(8):
                    self.mm(ph, self.WG[:, kc, fs], HT[:, kc, 0:514:513], start=(kc == 0), stop=(kc == 7),
                            R=[hk], W=[("ph4", s)], X=[("B", 4)])
                for kc in range(8):
                    self.mm(pu, self.WU[:, kc, fs], HT[:, kc, 1:513], start=(kc == 0), stop=(kc == 7),
                            R=[hk], W=[("pu4", s)], X=[("B", 2 + s)])
                Asb, ak = Asr.next()
                self.cp("act", Asb[:, 1:513], pa, R=[("pa4", s)], W=[(ak, "m")], X=[("B", s)])
                self.cp("act", Asb[:, 0:514:513], ph, R=[("ph4", s)], W=[(ak, "h")], X=[("B", 4)])
                C, ck = Cr.next()
                self.ts("pool", C, Asb[:, 1:513], CP[:, 1, fc:fc + 1], CP[:, 3, fc:fc + 1], ALU.mult, ALU.add,
                        R=[(ak, "m"), "CP"], W=[ck])
                self.stt(C, Asb[:, 0:512], CP[:, 0, fc:fc + 1], C, ALU.mult, ALU.add, R=[(ak, "m"), (ak, "h"), ck], W=[ck])
                self.stt(C, Asb[:, 2:514], CP[:, 2, fc:fc + 1], C, ALU.mult, ALU.add, R=[(ak, "m"), (ak, "h"), ck], W=[ck])
                G, gk = Gr.next()
                self.act(G, C, AF.Gelu, R=[ck], W=[gk])
                self.tt("dve", GU[:, fc, :], G, pu, ALU.mult, R=[gk, ("pu4", s)], W=[("GU", fc)], X=[("B", 2 + s)])
            for ti in range(4):
                n = sb * 4 + ti
                s = ti % 2
                x1, x1k = x1r.next()
                self.dma(x1, self.x1S[n * 128:(n + 1) * 128, :], W=[x1k], key=x1k)
                x2, x2k = x2r.next()
                for hf in range(2):
                    bk = 6 + hf
                    pdh = self.bank(bk)
                    for fc in range(NFC):
                        self.mm(pdh, GU[:, fc, ti * 128:(ti + 1) * 128], self.WD[:, fc, hf * 512:(hf + 1) * 512],
                                start=(fc == 0), stop=(fc == NFC - 1), R=[("GU", fc)], W=[("pd4", hf)], X=[("B", bk)])
                    self.tt("dve", x2[:, hf * 512:(hf + 1) * 512], pdh, x1[:, hf * 512:(hf + 1) * 512], ALU.add,
                            R=[("pd4", hf), x1k], W=[(x2k, hf)], X=[("B", bk)])
                self.dma(self.x2S[n * 128:(n + 1) * 128, :], x2, R=[(x2k, 0), (x2k, 1)], W=[("x2S", n)], key=x2k)
        self.P.barrier()
        A.release(self.m_ffn)

    def phase5(self):
        A = self.arena
        S, NT = self.S, self.NT
        m5 = A.mark()
        WPG = A.alloc([128, 8, D], BF16)
        WPL = A.alloc([128, 2, D], BF16)
        GF = A.alloc([128, D], F32)
        wst = Ring("wst5", [A.alloc([128, D], F32) for _ in range(2)])
        for kc in range(8):
            st, k = wst.next()
            self.dma(st, self.w_pleg[kc * 128:(kc + 1) * 128, :], W=[k], key=k)
            self.ts("pool" if kc % 2 else "dve", WPG[:, kc, :], st, self.gpk[:, 2, kc:kc + 1], 0.0, ALU.mult, ALU.add,
                    R=[k], W=[("WPG", kc)])
        for kc in range(2):
            st, k = wst.next()
            self.dma(st, self.w_ple[kc * 128:(kc + 1) * 128, :], W=[k], key=k)
            self.cp("pool", WPL[:, kc, :], st, R=[k], W=[("WPL", kc)])
        self.dma(GF, self.gfin, W=["GF"], key="GF")
        x2r = Ring("x2c", [A.alloc([128, D], F32) for _ in range(8)])
        ptr = Ring("pt5", [A.alloc([128, 256], F32) for _ in range(3)])
        pbr = Ring("pb5", [A.alloc([128, 256], BF16) for _ in range(4)])
        xnr = Ring("xn5", [A.alloc([128, D], BF16) for _ in range(3)])
        h3r = Ring("h3", [A.alloc([128, 8, 128], BF16) for _ in range(3)])
        pTr = Ring("pT", [A.alloc([128, 2, 128], BF16) for _ in range(4)])
        thr = Ring("th", [A.alloc([128, D], F32) for _ in range(3)])
        x3r = Ring("x3", [A.alloc([128, D], F32) for _ in range(4)])
        outr = Ring("outt", [A.alloc([128, D], F32) for _ in range(3)])
        ssr = Ring("ss5", [A.alloc([128, 1], F32) for _ in range(8)])
        rsr = Ring("rs5", [A.alloc([128, 1], F32) for _ in range(8)])
        junk = A.alloc([128, D], BF16)
        st5 = {}

        def fl(n):
            x2, x2k = x2r.next()
            pt, ptk = ptr.next()
            self.dma(x2, self.x2S[n * 128:(n + 1) * 128, :], W=[x2k], key=x2k)
            self.dma(pt, self.p_in[n * 128:(n + 1) * 128, :], W=[ptk], key=ptk)
            st5[n] = dict(x2=(x2, x2k), pt=(pt, ptk))

        def f0(n):
            e = st5[n]
            x2, x2k = e["x2"]
            pt, ptk = e["pt"]
            pb, pbk = pbr.next()
            self.cp("pool", pb, pt, R=[ptk], W=[pbk])
            ss, ssk = ssr.next()
            rs, rsk = rsr.next()
            self.act(junk, x2, AF.Square, R=[x2k], W=[ssk, "junk5"], accum=ss)
            self.rstd_from_ss(rs, ss, D, R=[ssk], W=[rsk], tmpkey=("rstmp5", rsk))
            e["pb"] = (pb, pbk)
            e["rs"] = (rs, rsk)

        def f0c(n):
            e = st5[n]
            x2, x2k = e["x2"]
            rs, rsk = e["rs"]
            xn, xnk = xnr.next()
            self.act(xn, x2, AF.Copy, R=[x2k, rsk], W=[xnk], scale=rs[:, 0:1])
            e["xn"] = (xn, xnk)

        def f1(n):
            e = st5[n]
            xn, xnk = e["xn"]
            pb, pbk = e["pb"]
            s = n % 2
            tb = 4 * s
            XT = [("B", tb)]
            tp = self.bank_bf(tb)
            for kc in range(8):
                self.tr(tp[:, kc, :], xn[:, kc * 128:(kc + 1) * 128], R=[xnk], W=[("tp5a", s)], X=XT)
            h3, h3k = h3r.next()
            self.cp("act", h3, tp, R=[("tp5a", s)], W=[h3k], X=XT)
            XP = [("B", tb + 1)]
            tq = self.bank_bf(tb + 1)
            for kc in range(2):
                self.tr(tq[:, kc, :], pb[:, kc * 128:(kc + 1) * 128], R=[pbk], W=[("tp5b", s)], X=XP)
            pT, pTk = pTr.next()
            self.cp("dve", pT, tq[:, 0:2, :], R=[("tp5b", s)], W=[pTk], X=XP)
            e["h3"] = (h3, h3k)
            e["pT"] = (pT, pTk)

        def f2(n):
            e = st5[n]
            h3, h3k = e["h3"]
            s = n % 2
            tb = 4 * s
            pg = self.bank(tb + 2, 2)
            XG = [("B", tb + 2), ("B", tb + 3)]
            for hf in range(2):
                for kc in range(8):
                    self.mm(pg[:, hf * 512:(hf + 1) * 512], h3[:, kc, :], WPG[:, kc, hf * 512:(hf + 1) * 512],
                            start=(kc == 0), stop=(kc == 7), R=[h3k, ("WPG", kc)], W=[("pg5", s)], X=[("B", tb + 2 + hf)])
            th, thk = thr.next()
            self.act(th, pg, AF.Tanh, R=[("pg5", s)], W=[thk], scale=0.5, X=XG)
            e["th"] = (th, thk)

        def f3(n):
            e = st5[n]
            pT, pTk = e["pT"]
            th, thk = e["th"]
            x2, x2k = e["x2"]
            s = n % 2
            tb = 4 * s
            pg = self.bank(tb + 2, 2)
            XG = [("B", tb + 2), ("B", tb + 3)]
            for hf in range(2):
                for kc in range(2):
                    self.mm(pg[:, hf * 512:(hf + 1) * 512], pT[:, kc, :], WPL[:, kc, hf * 512:(hf + 1) * 512],
                            start=(kc == 0), stop=(kc == 1), R=[pTk, ("WPL", kc)], W=[("pg5", s)], X=[("B", tb + 2 + hf)])
            x3, x3k = x3r.next()
            self.stt(x3, th, 1.0, pg, ALU.add, ALU.mult, R=[("pg5", s), thk], W=[x3k], X=XG)
            self.stt(x3, x3, 0.5, x2, ALU.mult, ALU.add, R=[x3k, x2k], W=[x3k])
            e["x3"] = (x3, x3k)

        def f4(n):
            e = st5[n]
            x3, x3k = e["x3"]
            ss, ssk = ssr.next()
            rs, rsk = rsr.next()
            self.act(junk, x3, AF.Square, R=[x3k], W=[ssk, "junk5"], accum=ss)
            self.rstd_from_ss(rs, ss, D, R=[ssk], W=[rsk], tmpkey=("rstmp5", rsk))
            e["rs2"] = (rs, rsk)

        def f5(n):
            e = st5.pop(n)
            x3, x3k = e["x3"]
            rs, rsk = e["rs2"]
            ot, otk = outr.next()
            self.stt(ot, x3, rs[:, 0:1], GF, ALU.mult, ALU.mult, R=[x3k, rsk, "GF"], W=[otk])
            self.dma(self.out[n * 128:(n + 1) * 128, :], ot, R=[otk], W=[("out", n)], key=otk)

        skew([fl, f0, f0c, f1, f2, f3, f4, f5], NT)
        self.P.barrier()
        A.release(m5)

    def build(self):
        self.setup()
        if 1 in self.phases:
            self.phase1()
        if 2 in self.phases:
            self.phase2a()
            self.phase2b()
        if 3 in self.phases:
            self.phase3()
        if 4 in self.phases:
            self.phase4()
        if 5 in self.phases:
            self.phase5()
        self.P.barrier()
        self.P.emit()
        return self.nc


def host_consts():
    c = {}
    c["c_ident"] = np.eye(128, dtype=np.float32)
    s = np.arange(128)[:, None]
    t = np.arange(128)[None, :]
    same = (s // 64) == (t // 64)
    tri = np.zeros((128, 4, 128), np.float32)
    tri[:, 0, :] = same & (s <= t)
    tri[:, 1, :] = same & (s > t)
    tri[:, 2, :] = same & (s >= t)
    tri[:, 3, :] = same & (s < t)
    c["c_tri"] = tri
    ind = np.zeros((128, 2), np.float32)
    ind[:64, 0] = 1
    ind[64:, 1] = 1
    c["c_ind"] = ind
    s = np.arange(64)[:, None]
    t = np.arange(64)[None, :]
    mk = np.zeros((64, 2, 64), np.float32)
    mk[:, 0, :] = s <= t
    mk[:, 1, :] = s >= t
    c["c_mask"] = mk
    half = 8
    invf = (500000.0 ** (-np.arange(half, dtype=np.float32) / half)).astype(np.float32)
    c["c_invf"] = np.ascontiguousarray(np.broadcast_to(invf[None, :], (128, 8))).astype(np.float32)
    return c


def host_params(inp, S):
    f = lambda a: np.ascontiguousarray(np.asarray(a), dtype=np.float32)
    m = {}
    m["w_in"] = f(inp["w_in"][0])
    m["w_out"] = f(inp["w_out"][0])
    m["w_gate"] = f(inp["w_ffn_gate"][0])
    m["w_up"] = f(inp["w_ffn_up"][0])
    m["w_down"] = f(inp["w_ffn_down"][0])
    m["w_ple"] = f(inp["w_ple"][0])
    m["w_pleg"] = f(inp["w_ple_gate"][0])
    g3 = np.stack([np.asarray(inp["norm_mix_g"][0]), np.asarray(inp["norm_ffn_g"][0]),
                   np.asarray(inp["norm_ple_g"][0])], 0)
    m["g_pk"] = f(g3.reshape(3, 8, 128).transpose(2, 0, 1))
    rep = lambda a, shape: f(np.broadcast_to(np.asarray(a)[None], (128,) + tuple(shape)))
    m["gam"] = rep(inp["hg_lb_gamma"], (2, 2, 512))
    lv = np.stack([np.asarray(inp["lam_q1"][0]), np.asarray(inp["lam_k1"][0]),
                   np.asarray(inp["lam_q2"][0]), np.asarray(inp["lam_k2"][0])], 0)
    m["lamv"] = rep(lv, (4, 64))
    m["gsub"] = rep(inp["da_subln_g"][0], (128,))
    m["ghg"] = rep(inp["hg_norm_g"][0], (128,))
    m["gfin"] = rep(inp["final_norm_g"], (D,))
    cw = np.concatenate([np.asarray(inp["ffn_conv_w"][0]), np.asarray(inp["ffn_conv_b"][0])[None]], 0)
    m["convp"] = f(cw.reshape(4, NFC, 128).transpose(2, 0, 1))
    m.update(host_consts())
    return m


def core_inputs(inp, b, S, shared):
    m = dict(shared)
    m["x"] = np.ascontiguousarray(np.asarray(inp["x"][b], dtype=np.float32))
    m["p"] = np.ascontiguousarray(np.asarray(inp["p"][0, b], dtype=np.float32))
    pos = np.asarray(inp["positions"][b]).astype(np.int32)
    m["pos"] = np.ascontiguousarray(pos.reshape(S // 128, 128).T)
    return m


_CACHE = {}


def kernel(**inputs):
    B, S = inputs["x"].shape[0], inputs["x"].shape[1]
    if S not in _CACHE:
        _CACHE[S] = KB(S).build()
    nc = _CACHE[S]
    shared = host_params(inputs, S)
    in_maps = [core_inputs(inputs, b, S, shared) for b in range(B)]
    res = run_bass_kernel_spmd(nc, in_maps, core_ids=list(range(B)))
    return np.stack([np.asarray(r["out"], dtype=np.float32) for r in res.results], 0)
```

```python
import contextlib
import math
import numpy as np
import concourse.bass as bass
import concourse.mybir as mybir
from concourse.bass_utils import run_bass_kernel_spmd

F32 = mybir.dt.float32
BF16 = mybir.dt.bfloat16
I32 = mybir.dt.int32
U8 = mybir.dt.uint8
AF = mybir.ActivationFunctionType
ALU = mybir.AluOpType
AX = mybir.AxisListType
DTS = {F32: 4, BF16: 2, I32: 4, U8: 1}

ENGS = ("pe", "act", "dve", "pool", "sp")
D = 1024
DFF = 2816
NFC = DFF // 128
EPS = 1e-6
ARENA_BYTES = 207 * 1024


class Op:
    __slots__ = ("eng", "fn", "raw", "war", "dma", "sig", "idx", "dval", "need")

    def __init__(self, eng, fn, dma):
        self.eng = eng
        self.fn = fn
        self.raw = set()
        self.war = set()
        self.dma = dma
        self.sig = 0
        self.dval = 0
        self.need = False


class Prog:
    def __init__(self, nc):
        self.nc = nc
        self.ops = []
        self.q = {e: [] for e in ENGS}
        self.lastw = {}
        self.readers = {}
        self.dma_cnt = {}
        self.last_dma = {}
        self.last_real = {}
        self.lastx = {}
        self.phys = {}
        self.phys_cnt = []
        self.stack = contextlib.ExitStack()
        self.ntile = 0

    def sbuf(self, shape, dtype, name=None):
        self.ntile += 1
        return self.stack.enter_context(
            self.nc.sbuf_tensor(name or f"sb{self.ntile}", list(shape), dtype))

    def psum(self, shape, dtype, name=None):
        self.ntile += 1
        return self.stack.enter_context(
            self.nc.psum_tensor(name or f"ps{self.ntile}", list(shape), dtype))

    def op(self, eng, fn, reads=(), writes=(), dma=None, excl=()):
        o = Op(eng, fn, dma)
        o.idx = len(self.ops)
        for x in excl:
            la = self.lastx.get(x)
            if la is not None and (la.eng != eng or la.dma is not None or dma is not None):
                o.raw.add(la)
            self.lastx[x] = o
        for r in reads:
            w = self.lastw.get(r)
            if w is not None:
                o.raw.add(w)
        for w in writes:
            lw = self.lastw.get(w)
            if lw is not None:
                o.raw.add(lw)
            for rd in self.readers.get(w, ()):
                if rd is not o:
                    o.war.add(rd)
        for w in writes:
            self.lastw[w] = o
            self.readers[w] = []
        for r in reads:
            if r not in writes:
                self.readers.setdefault(r, []).append(o)
        if dma is not None:
            pi = self.phys.get(dma)
            if pi is None:
                pi = len(self.phys)
                self.phys[dma] = pi
                if pi >= len(self.phys_cnt):
                    self.phys_cnt.append(0)
            self.phys_cnt[pi] += 16
            o.dma = pi
            o.dval = self.phys_cnt[pi]
            self.last_dma[pi] = o
        self.ops.append(o)
        self.q[eng].append(o)
        if fn is not None:
            self.last_real[eng] = o
        return o

    def barrier(self):
        deps = list(self.last_real.values()) + list(self.last_dma.values())
        for e in ENGS:
            o = Op(e, None, None)
            o.idx = len(self.ops)
            o.raw.update(deps)
            self.ops.append(o)
            self.q[e].append(o)
        self.lastw.clear()
        self.readers.clear()
        self.lastx.clear()
        self.phys.clear()
        self.last_dma.clear()

    def _edges(self, o):
        for p in o.raw:
            if p.dma is None and p.eng == o.eng and o.eng == "pe" and o.dma is None and o.fn is not None:
                continue
            yield p
        for p in o.war:
            if p.dma is None and p.eng == o.eng and o.dma is None:
                continue
            yield p

    def emit(self):
        nc = self.nc
        for o in self.ops:
            for p in self._edges(o):
                p.need = True
        cnt = {e: 0 for e in ENGS}
        for o in self.ops:
            if o.dma is None and o.need:
                cnt[o.eng] += 1
                o.sig = cnt[o.eng]
        st = self.stack
        esem = {e: st.enter_context(nc.semaphore(f"s_{e}")) for e in ENGS}
        dsem = {i: st.enter_context(nc.semaphore(f"d_{i}")) for i in range(len(self.phys_cnt))}
        self.nsig = cnt
        prog = self

        def run(ename, eng):
            waited = {}
            for o in prog.q[ename]:
                for p in prog._edges(o):
                    if p.dma is not None:
                        s, v = dsem[p.dma], p.dval
                    else:
                        s, v = esem[p.eng], p.sig
                    key = id(s)
                    if waited.get(key, 0) >= v:
                        continue
                    waited[key] = v
                    eng.wait_ge(s, v)
                if o.fn is None:
                    continue
                ins = o.fn(eng)
                if o.dma is not None:
                    ins.then_inc(dsem[o.dma], 16)
                elif o.need:
                    ins.then_inc(esem[ename], 1)

        with nc.Block() as block:
            @block.tensor
            def _(e):
                run("pe", e)

            @block.scalar
            def _(e):
                run("act", e)

            @block.vector
            def _(e):
                run("dve", e)

            @block.gpsimd
            def _(e):
                run("pool", e)

            @block.sync
            def _(e):
                run("sp", e)
        st.close()


class Arena:
    def __init__(self, ap, size):
        self.ap = ap
        self.size = size
        self.off = 0

    def alloc(self, shape, dt, parts=128):
        n = 1
        for s in shape[1:]:
            n *= s
        nb = n * DTS[dt]
        nb_al = (nb + 31) // 32 * 32
        assert self.off + nb_al <= self.size, f"arena overflow {self.off}+{nb_al}>{self.size}"
        v = self.ap[0:shape[0], self.off:self.off + nb].bitcast(dt)
        self.off += nb_al
        if len(shape) == 3:
            v = v.rearrange("p (a b) -> p a b", a=shape[1])
        elif len(shape) == 4:
            v = v.rearrange("p (a b c) -> p a b c", a=shape[1], b=shape[2])
        return v

    def mark(self):
        return self.off

    def release(self, m):
        self.off = m


def skew(stages, n):
    K = len(stages)
    for t in range(n + K - 1):
        for k in range(K):
            i = t - k
            if 0 <= i < n:
                stages[k](i)


class Ring:
    def __init__(self, name, tiles):
        self.name = name
        self.tiles = tiles
        self.i = 0

    def next(self):
        j = self.i % len(self.tiles)
        self.i += 1
        return self.tiles[j], (self.name, j)


class KB:
    def __init__(self, S, dbg=False, phases=(1, 2, 3, 4, 5)):
        self.S = S
        self.NT = S // 128
        self.NCH = S // 64
        self.dbg = dbg
        self.phases = phases
        nc = bass.Bass("TRN2", target_bir_lowering=False)
        self.nc = nc
        self.P = Prog(nc)
        self.arena = Arena(self.P.sbuf([128, ARENA_BYTES], U8, "arena")[:, :], ARENA_BYTES)
        self.psall = self.P.psum([128, 4096], F32, "psall")
        self.inputs()
        self.scratch()

    def din(self, name, shape, dt=F32):
        return self.nc.dram_tensor(name, list(shape), dt, kind="ExternalInput").ap()

    def inputs(self):
        S, NT = self.S, self.NT
        self.x = self.din("x", [S, D])
        self.p_in = self.din("p", [S, 256])
        self.pos = self.din("pos", [128, NT], I32)
        self.w_in = self.din("w_in", [D, 4096])
        self.w_out = self.din("w_out", [D, D])
        self.w_gate = self.din("w_gate", [D, DFF])
        self.w_up = self.din("w_up", [D, DFF])
        self.w_down = self.din("w_down", [DFF, D])
        self.w_ple = self.din("w_ple", [256, D])
        self.w_pleg = self.din("w_pleg", [D, D])
        self.g_pk = self.din("g_pk", [128, 3, 8])
        self.gam = self.din("gam", [128, 2, 2, 512])
        self.lamv = self.din("lamv", [128, 4, 64])
        self.gsub = self.din("gsub", [128, 128])
        self.ghg = self.din("ghg", [128, 128])
        self.gfin = self.din("gfin", [128, D])
        self.convp = self.din("convp", [128, 4, NFC])
        self.c_ident = self.din("c_ident", [128, 128])
        self.c_tri = self.din("c_tri", [128, 4, 128])
        self.c_ind = self.din("c_ind", [128, 2])
        self.c_mask = self.din("c_mask", [64, 2, 64])
        self.c_invf = self.din("c_invf", [128, 8])
        self.out = self.nc.dram_tensor("out", [S, D], F32, kind="ExternalOutput").ap()

    def dscr(self, name, shape, dt):
        kind = "ExternalOutput" if self.dbg else "Internal"
        return self.nc.dram_tensor(name, list(shape), dt, kind=kind).ap()

    def scratch(self):
        S = self.S
        self.qT = self.dscr("s_qT", [512, S], BF16)
        self.kT = self.dscr("s_kT", [512, S], BF16)
        self.vS = self.dscr("s_v", [S, 512], BF16)
        self.hqT = self.dscr("s_hqT", [2, 512, S], BF16)
        self.hkT = self.dscr("s_hkT", [2, 512, S], BF16)
        self.hkd = self.dscr("s_hkd", [2, S, 512], BF16)
        self.hvS = self.dscr("s_hv", [S, 512], BF16)
        self.gateS = self.dscr("s_gate", [S, 512], F32)
        self.ofS = self.dscr("s_of", [S, 512], F32)
        self.mixS = self.dscr("s_mix", [S, D], BF16)
        self.x1S = self.dscr("s_x1", [S, D], F32)
        self.h2T = self.dscr("s_h2T", [D, S + 2], BF16)
        self.x2S = self.dscr("s_x2", [S, D], F32)

    def bank(self, i, n=1):
        return self.psall[:, i * 512:(i + n) * 512]

    def bank_bf(self, i, a=8):
        return self.bank(i).bitcast(BF16).rearrange("p (a b) -> p a b", a=a)

    def mm(self, out, lhsT, rhs, start=True, stop=True, R=(), W=(), X=(), **kw):
        self.P.op("pe", lambda e: e.matmul(out, lhsT=lhsT, rhs=rhs, start=start, stop=stop, **kw), R, W, excl=X)

    def tr(self, out, in_, R=(), W=(), X=(), ident=None):
        ident = self.ident if ident is None else ident
        self.P.op("pe", lambda e: e.transpose(out=out, in_=in_, identity=ident), R, W, excl=X)

    def act(self, out, in_, func, R=(), W=(), scale=1.0, bias=None, accum=None, X=()):
        def fn(e):
            kw = {}
            if bias is not None:
                kw["bias"] = bias
            if accum is not None:
                kw["accum_out"] = accum
            return e.activation(out=out, in_=in_, func=func, scale=scale, **kw)
        self.P.op("act", fn, R, W, excl=X)

    def ts(self, eng, out, in0, s1, s2, op0, op1=None, R=(), W=(), accum=None, X=()):
        def fn(e):
            kw = {}
            if accum is not None:
                kw["accum_out"] = accum
            if op1 is None:
                return e.tensor_scalar(out=out, in0=in0, scalar1=s1, scalar2=None, op0=op0, **kw)
            return e.tensor_scalar(out=out, in0=in0, scalar1=s1, scalar2=s2, op0=op0, op1=op1, **kw)
        self.P.op(eng, fn, R, W, excl=X)

    def tt(self, eng, out, in0, in1, op, R=(), W=(), X=()):
        self.P.op(eng, lambda e: e.tensor_tensor(out=out, in0=in0, in1=in1, op=op), R, W, excl=X)

    def stt(self, out, in0, scalar, in1, op0, op1, R=(), W=(), accum=None, X=()):
        def fn(e):
            kw = {}
            if accum is not None:
                kw["accum_out"] = accum
            return e.scalar_tensor_tensor(out=out, in0=in0, scalar=scalar, in1=in1, op0=op0, op1=op1, **kw)
        self.P.op("dve", fn, R, W, excl=X)

    def cp(self, eng, out, in_, R=(), W=(), X=()):
        if eng == "act":
            self.P.op("act", lambda e: e.copy(out=out, in_=in_), R, W, excl=X)
        else:
            self.P.op(eng, lambda e: e.tensor_copy(out=out, in_=in_), R, W, excl=X)

    def recip(self, out, in_, R=(), W=(), X=()):
        self.P.op("dve", lambda e: e.reciprocal(out=out, in_=in_), R, W, excl=X)

    def memset(self, eng, ap, val, R=(), W=()):
        self.P.op(eng, lambda e: e.memset(ap, val), R, W)

    def dma(self, out, in_, R=(), W=(), key=None, eng="sp", slow=False):
        if slow:
            self.P.op(eng, lambda e: e.dma_start(out=out, in_=in_, allow_slow_non_contiguous=True), R, W, dma=key)
        else:
            self.P.op(eng, lambda e: e.dma_start(out=out, in_=in_), R, W, dma=key)

    def rstd_from_ss(self, rstd, ss, n, R, W, tmpkey):
        mh = self.mhalf[0:rstd.shape[0], 0:1]
        if rstd.shape[1] != 1:
            mh = mh.to_broadcast([rstd.shape[0], rstd.shape[1]])
        self.ts("pool", rstd, ss, 1.0 / n, EPS, ALU.mult, ALU.add, R=R, W=[tmpkey])
        self.tt("pool", rstd, rstd, mh, ALU.pow, R=[tmpkey, "mhalf"], W=W)

    def setup(self):
        A = self.arena
        S, NT = self.S, self.NT
        self.identf = A.alloc([128, 128], F32)
        self.ident = A.alloc([128, 128], BF16)
        self.tri = A.alloc([128, 4, 128], F32)
        self.ind = A.alloc([128, 2], F32)
        self.mask = A.alloc([64, 2, 64], F32)
        self.mhalf = A.alloc([128, 1], F32)
        self.gpk = A.alloc([128, 3, 8], F32)
        self.GS = A.alloc([128, 128], F32)
        self.GHG = A.alloc([128, 128], F32)
        self.lamneg = A.alloc([128, 1], F32)
        self.EDEC = A.alloc([128, 2, 4, self.NCH], F32)
        self.zero = A.alloc([128, 16], BF16)
        ld = [(self.identf, self.c_ident), (self.tri, self.c_tri), (self.ind, self.c_ind),
              (self.mask, self.c_mask), (self.gpk, self.g_pk), (self.GS, self.gsub),
              (self.GHG, self.ghg)]
        for i, (dst, src) in enumerate(ld):
            self.dma(dst, src, W=[("c", i)], key=("c", i))
        self.cp("dve", self.ident, self.identf, R=[("c", 0)], W=["ident"])
        self.memset("pool", self.mhalf, -0.5, W=["mhalf"])
        self.memset("pool", self.zero, 0.0, W=["zero"])
        self.ts("dve", self.GS, self.GS, 0.8, None, ALU.mult, R=[("c", 5)], W=[("c", 5)])
        m = A.mark()
        lv = A.alloc([128, 4, 64], F32)
        pr = A.alloc([128, 2, 64], F32)
        sm = A.alloc([128, 2], F32)
        self.dma(lv, self.lamv, W=["lv"], key="lv")
        self.tt("dve", pr, lv[:, 0:4:2, :], lv[:, 1:4:2, :], ALU.mult, R=["lv"], W=["pr"])
        self.P.op("dve", lambda e: e.tensor_reduce(out=sm, in_=pr, axis=AX.X, op=ALU.add), ["pr"], ["sm"])
        self.act(sm, sm, AF.Exp, R=["sm"], W=["sm"])
        self.stt(self.lamneg, sm[:, 1:2], -0.2, sm[:, 0:1], ALU.add, ALU.subtract, R=["sm"], W=["lamneg"])
        A.release(m)

    def phase1(self):
        A = self.arena
        S, NT = self.S, self.NT
        m0 = A.mark()
        WIN = A.alloc([128, 8, 4096], BF16)
        wst = Ring("wst", [A.alloc([128, 1024], F32) for _ in range(2)])
        LB0 = A.alloc([128, 2, 512], F32)
        LB1 = A.alloc([128, 2, 512], F32)
        COS = A.alloc([128, NT, 8], F32)
        SIN = A.alloc([128, NT, 8], F32)
        m1 = A.mark()
        gm = A.alloc([128, 2, 2, 512], F32)
        self.dma(gm, self.gam, W=["gm"], key="gm")
        self.tt("dve", LB0, gm[:, :, 1, :], gm[:, :, 0, :], ALU.subtract, R=["gm"], W=["LB0"])
        self.act(LB0, LB0, AF.Exp, R=["LB0"], W=["LB0"])
        self.ts("dve", LB0, LB0, 1.0, None, ALU.add, R=["LB0"], W=["LB0"])
        self.recip(LB0, LB0, R=["LB0"], W=["LB0"])
        self.ts("dve", LB1, LB0, -1.0, 1.0, ALU.mult, ALU.add, R=["LB0"], W=["LB1"])
        posi = A.alloc([128, NT], I32)
        posf = A.alloc([128, NT], F32)
        invf = A.alloc([128, 8], F32)
        ang = A.alloc([128, NT, 8], F32)
        self.dma(posi, self.pos, W=["posi"], key="posi")
        self.dma(invf, self.c_invf, W=["invf"], key="invf")
        self.cp("dve", posf, posi, R=["posi"], W=["posf"])
        self.tt("dve", ang, posf.unsqueeze(2).to_broadcast([128, NT, 8]),
                invf.unsqueeze(1).to_broadcast([128, NT, 8]), ALU.mult, R=["posf", "invf"], W=["ang"])
        TWO_PI = 2.0 * math.pi
        C1 = 6.28125
        C2 = TWO_PI - C1
        PI_LO = 3.1415925
        ti = A.alloc([128, NT, 8], I32)
        tf = A.alloc([128, NT, 8], F32)
        for dst, dk, add in ((SIN, "SIN", 0.0), (COS, "COS", 0.5 * math.pi)):
            self.ts("dve", tf, ang, 1.0 / TWO_PI, add / TWO_PI, ALU.mult, ALU.add, R=["ang"], W=["tf"])
            self.cp("dve", ti, tf, R=["tf"], W=["ti"])
            self.cp("dve", tf, ti, R=["ti"], W=["tf"])
            self.ts("dve", dst, ang, add, None, ALU.add, R=["ang"], W=[dk])
            self.stt(dst, tf, -C1, dst, ALU.mult, ALU.add, R=["tf", dk], W=[dk])
            self.stt(dst, tf, -C2, dst, ALU.mult, ALU.add, R=["tf", dk], W=[dk])
            self.ts("dve", dst, dst, -PI_LO, PI_LO, ALU.max, ALU.min, R=[dk], W=[dk])
            self.act(dst, dst, AF.Sin, R=[dk], W=[dk])
        A.release(m1)
        i = 0
        for kc in range(8):
            for cb in range(4):
                st, k = wst.next()
                self.dma(st, self.w_in[kc * 128:(kc + 1) * 128, cb * 1024:(cb + 1) * 1024], W=[k], key=k)
                dst = WIN[:, kc, cb * 1024:(cb + 1) * 1024]
                g = self.gpk[:, 0, kc:kc + 1]
                if i % 2 == 0:
                    self.ts("dve", dst, st, g, None, ALU.mult, R=[k, ("c", 4)], W=[("WIN", kc, cb)])
                else:
                    self.ts("pool", dst, st, g, 0.0, ALU.mult, ALU.add, R=[k, ("c", 4)], W=[("WIN", kc, cb)])
                i += 1
        xt = Ring("xt", [A.alloc([128, D], F32) for _ in range(4)])
        jkr = Ring("junk1", [A.alloc([128, D], BF16) for _ in range(1)])
        ssr = Ring("ss", [A.alloc([128, 1], F32) for _ in range(4)])
        rsr = Ring("rs", [A.alloc([128, 1], F32) for _ in range(4)])
        xnr = Ring("xn", [A.alloc([128, D], BF16) for _ in range(2)])
        hTr = Ring("hT", [A.alloc([128, 8, 128], BF16) for _ in range(2)])
        qkr = Ring("qk", [A.alloc([128, 16, 64], BF16) for _ in range(2)])
        rt = Ring("rt", [A.alloc([128, 16, 8], F32) for _ in range(4)])
        qkTs = Ring("qkTs", [A.alloc([128, 8, 256], BF16) for _ in range(2)])
        vsr = Ring("vs", [A.alloc([128, 512], BF16) for _ in range(2)])
        hvr = Ring("hvs", [A.alloc([128, 512], BF16) for _ in range(2)])
        w32 = Ring("w32", [A.alloc([128, 512], F32) for _ in range(6)])
        lfr = Ring("lfr", [A.alloc([128, 512], F32) for _ in range(4)])
        kinr = Ring("kinr", [A.alloc([128, 512], F32) for _ in range(4)])
        hqr = Ring("hq", [A.alloc([128, 512], F32) for _ in range(2)])
        sgr = Ring("sg", [A.alloc([128, 512], F32) for _ in range(2)])
        b16 = Ring("b16", [A.alloc([128, 512], BF16) for _ in range(12)])
        kdr = Ring("kd", [A.alloc([128, 512], BF16) for _ in range(4)])
        hTs = [Ring(f"hTs{d}", [A.alloc([128, 8, 256], BF16) for _ in range(2)]) for d in range(2)]
        tpr = Ring("tpb", [(self.bank_bf(0), 0), (self.bank_bf(7), 7)])
        pqk = self.bank(1, 2)
        XQK = [("B", 1), ("B", 2)]
        pzr = Ring("pz", [(self.bank(3), 3), (self.bank(4), 4)])
        ptA = self.bank(5)
        ptB = self.bank(6)
        XA = [("B", 5)]
        XB = [("B", 6)]
        xn_cur = {}

        xload = {}

        def load_x(n):
            x_t, xk = xt.next()
            self.dma(x_t, self.x[n * 128:(n + 1) * 128, :], W=[xk], key=xk)
            xload[n] = (x_t, xk)

        rms_cur = {}

        def rmsA(n):
            if n == 0:
                load_x(0)
            if n + 1 < NT:
                load_x(n + 1)
            x_t, xk = xload.pop(n)
            ss, sk = ssr.next()
            rs, rk = rsr.next()
            junk, jk = jkr.next()
            self.act(junk, x_t, AF.Square, R=[xk], W=[sk, jk], accum=ss)
            self.rstd_from_ss(rs, ss, D, R=[sk], W=[rk], tmpkey=("rstmp", rk))
            rms_cur[n] = (x_t, xk, rs, rk)

        def rmsB(n):
            x_t, xk, rs, rk = rms_cur.pop(n)
            xn, nk = xnr.next()
            self.act(xn, x_t, AF.Copy, R=[xk, rk], W=[nk], scale=rs[:, 0:1])
            xn_cur[n] = (xn, nk)

        def proj(out, cg, hT, hk, W, X):
            for kc in range(8):
                self.mm(out, hT[:, kc, :], WIN[:, kc, cg * 512:(cg + 1) * 512], start=(kc == 0),
                        stop=(kc == 7), R=[hk, ("WIN", kc, cg // 2)], W=W, X=X)

        stage_state = {}

        def tstage(ring, key, n, srcs, dst_fn, eng):
            j = n % 2
            if j == 0:
                stage_state[key] = ring.next()
            stg, sk = stage_state[key]
            (pt, bk), pk = tpr.next()
            X = [("B", bk)]
            for i, (src, srck) in enumerate(srcs):
                self.tr(pt[:, i, :], src, R=[srck, "ident"], W=[pk], X=X)
            self.cp(eng, stg[:, :, j * 128:(j + 1) * 128], pt, R=[pk], W=[sk], X=X)
            if j == 1 or n == NT - 1:
                dst_fn(stg, sk, n - j, (j + 1) * 128)

        carry = {}
        carryB = {}

        def mainA(n):
            xn, nk = xn_cur.pop(n)
            (tp, bk), tpk = tpr.next()
            XT = [("B", bk)]
            for kc in range(8):
                self.tr(tp[:, kc, :], xn[:, kc * 128:(kc + 1) * 128], R=[nk, "ident"], W=[tpk], X=XT)
            hT, hk = hTr.next()
            self.cp("act", hT, tp, R=[tpk], W=[hk], X=XT)
            kin = {}
            lfs = {}
            for d in range(2):
                (pz, bz), pk = pzr.next()
                XZ = [("B", bz)]
                proj(pz, 4 + d, hT, hk, [pk], XZ)
                E, ek = kinr.next()
                self.act(E, pz, AF.Exp, R=[pk], W=[ek], scale=-1.0, X=XZ)
                self.ts("pool", E, E, 1.0, 1.0, ALU.mult, ALU.add, R=[ek], W=[ek])
                kin[d] = (E, ek)
            for d in range(2):
                E, ek = kin[d]
                self.recip(E, E, R=[ek], W=[ek])
                self.tt("dve", E, E, LB1[:, d, :], ALU.mult, R=[ek, "LB1"], W=[ek])
                self.tt("pool", E, E, LB0[:, d, :], ALU.add, R=[ek, "LB0"], W=[ek])
            (pz, bz), pk = pzr.next()
            XZ = [("B", bz)]
            proj(pz, 3, hT, hk, [pk], XZ)
            HQ, hqk = hqr.next()
            self.cp("act", HQ, pz, R=[pk], W=[hqk], X=XZ)
            proj(pqk[:, 0:512], 0, hT, hk, ["pqk0"], [("B", 1)])
            proj(pqk[:, 512:1024], 1, hT, hk, ["pqk1"], [("B", 2)])
            qk, qkk = qkr.next()
            pq3 = pqk.rearrange("p (a b) -> p a b", a=16)
            self.cp("act", qk, pq3, R=["pqk0", "pqk1"], W=[qkk], X=XQK)
            cosb = COS[:, n:n + 1, :].to_broadcast([128, 16, 8])
            sinb = SIN[:, n:n + 1, :].to_broadcast([128, 16, 8])
            x1 = pq3[:, :, 0:8]
            x2 = pq3[:, :, 8:16]
            t1, k1 = rt.next()
            t2, k2 = rt.next()
            t3, k3 = rt.next()
            t4, k4 = rt.next()
            self.tt("dve", t1, x1, cosb, ALU.mult, R=["pqk0", "pqk1", "COS"], W=[k1], X=XQK)
            self.tt("dve", t2, x2, sinb, ALU.mult, R=["pqk0", "pqk1", "SIN"], W=[k2], X=XQK)
            self.tt("dve", t3, x2, cosb, ALU.mult, R=["pqk0", "pqk1", "COS"], W=[k3], X=XQK)
            self.tt("dve", t4, x1, sinb, ALU.mult, R=["pqk0", "pqk1", "SIN"], W=[k4], X=XQK)
            self.tt("pool", qk[:, :, 0:8], t1, t2, ALU.subtract, R=[k1, k2, qkk], W=[qkk])
            self.tt("pool", qk[:, :, 8:16], t3, t4, ALU.add, R=[k3, k4, qkk], W=[qkk])
            (pz, bz), pk = pzr.next()
            XZ = [("B", bz)]
            proj(pz, 2, hT, hk, [pk], XZ)
            vs, vk = vsr.next()
            self.cp("act", vs, pz, R=[pk], W=[vk], X=XZ)
            self.dma(self.vS[n * 128:(n + 1) * 128, :], vs, R=[vk], W=[("vS", n)], key=vk)
            (pz, bz), pk = pzr.next()
            XZ = [("B", bz)]
            proj(pz, 6, hT, hk, [pk], XZ)
            hv, hvk = hvr.next()
            self.cp("act", hv, pz, R=[pk], W=[hvk], X=XZ)
            self.dma(self.hvS[n * 128:(n + 1) * 128, :], hv, R=[hvk], W=[("hvS", n)], key=hvk)
            (pz, bz), pk = pzr.next()
            XZ = [("B", bz)]
            proj(pz, 7, hT, hk, [pk], XZ)
            E, ek = w32.next()
            self.act(E, pz, AF.Exp, R=[pk], W=[ek], scale=-1.0, X=XZ)
            self.ts("pool", E, E, 1.0, 1.0, ALU.mult, ALU.add, R=[ek], W=[ek])
            self.recip(E, E, R=[ek], W=[ek])
            SG, sgk = sgr.next()
            self.tt("dve", SG, pz, E, ALU.mult, R=[pk, ek], W=[sgk], X=XZ)
            self.dma(self.gateS[n * 128:(n + 1) * 128, :], SG, R=[sgk], W=[("gateS", n)], key=sgk)
            for d in range(2):
                E, ek = kin[d]
                LF, lk = lfr.next()
                self.act(LF, E, AF.Ln, R=[ek], W=[lk])
                self.ts("pool", E, E, -1.0, 1.0, ALU.mult, ALU.add, R=[ek, lk], W=[ek])
                lfs[d] = (LF, lk)
            carry[n] = (qk, qkk, lfs, kin, HQ, hqk)

        def mainB1(n):
            qk, qkk, lfs, kin, HQ, hqk = carry.pop(n)
            pbts = []
            for d in range(2):
                LF, lk = lfs[d]
                pbt = ptA[:, 8 * d:8 * d + 8].rearrange("p (a b) -> p a b", a=4)
                for h in range(4):
                    self.mm(pbt[:, h, :], LF[:, h * 128:(h + 1) * 128], self.ind, R=[lk, ("c", 2)], W=[("pbt", d)], X=XA)
                pbts.append(pbt)
            for d in range(2):
                self.act(self.EDEC[:, d, :, 2 * n:2 * n + 2], pbts[d], AF.Exp, R=[("pbt", d)], W=[("EDEC", d, n)], X=XA)
            qk2 = qk.rearrange("p a b -> p (a b)")

            def st_qk(stg, sk, n0, w):
                self.dma(self.qT.rearrange("(h p) t -> p h t", p=128)[:, :, n0 * 128:n0 * 128 + w],
                         stg[:, 0:4, 0:w], R=[sk], W=[("qT", n0)], key=(sk, "q"))
                self.dma(self.kT.rearrange("(h p) t -> p h t", p=128)[:, :, n0 * 128:n0 * 128 + w],
                         stg[:, 4:8, 0:w], R=[sk], W=[("kT", n0)], key=(sk, "k"))
            tstage(qkTs, "qk", n, [(qk2[:, i * 128:(i + 1) * 128], qkk) for i in range(8)], st_qk, "act")
            prods = {}
            for d in range(2):
                LF, lk = lfs[d]
                KIN, kk = kin[d]
                if d == 0:
                    pA, XAd, kA, pB, XBd, kB = ptA, XA, "ptA", ptB, XB, "ptB"
                else:
                    (pA, b1), kA = pzr.next()
                    (pB, b2), kB = pzr.next()
                    XAd, XBd = [("B", b1)], [("B", b2)]
                self.mm(pA, self.tri[:, 2 * d, :], LF, R=[lk, ("c", 1)], W=[kA], X=XAd)
                self.mm(pB, self.tri[:, 2 * d + 1, :], LF, R=[lk, ("c", 1)], W=[kB], X=XBd)
                X1, xk1 = w32.next()
                self.act(X1, pA, AF.Exp, R=[kA], W=[xk1], X=XAd)
                QT, qtk = b16.next()
                self.tt("dve", QT, HQ, X1, ALU.mult, R=[hqk, xk1], W=[qtk])
                X2, xk2 = w32.next()
                self.act(X2, pA, AF.Exp, R=[kA], W=[xk2], scale=-1.0, X=XAd)
                KT, ktk = b16.next()
                self.tt("dve", KT, KIN, X2, ALU.mult, R=[kk, xk2], W=[ktk])
                X3, xk3 = w32.next()
                self.act(X3, pB, AF.Exp, R=[kB], W=[xk3], X=XBd)
                KD, kdk = kdr.next()
                self.tt("pool", KD, KIN, X3, ALU.mult, R=[kk, xk3], W=[kdk])
                self.dma(self.hkd[d, n * 128:(n + 1) * 128, :], KD, R=[kdk], W=[("hkd", d, n)], key=kdk)
                prods[d] = (QT, qtk, KT, ktk)
            carryB[n] = prods

        def mainB2(n):
            prods = carryB.pop(n)
            for d in range(2):
                QT, qtk, KT, ktk = prods[d]

                def st_h(stg, sk, n0, w, d=d):
                    self.dma(self.hqT[d].rearrange("(h p) t -> p h t", p=128)[:, :, n0 * 128:n0 * 128 + w],
                             stg[:, 0:4, 0:w], R=[sk], W=[("hqT", d, n0)], key=(sk, "q"))
                    self.dma(self.hkT[d].rearrange("(h p) t -> p h t", p=128)[:, :, n0 * 128:n0 * 128 + w],
                             stg[:, 4:8, 0:w], R=[sk], W=[("hkT", d, n0)], key=(sk, "k"))
                srcs = [(QT[:, i * 128:(i + 1) * 128], qtk) for i in range(4)] + \
                       [(KT[:, i * 128:(i + 1) * 128], ktk) for i in range(4)]
                tstage(hTs[d], ("h", d), n, srcs, st_h, "act" if d == 0 else "dve")

        skew([rmsA, rmsB, mainA, mainB1, mainB2], NT)
        self.P.barrier()
        A.release(m0)

    def phase2a(self):
        A = self.arena
        S, NT = self.S, self.NT
        QB = 512
        NQB = S // QB
        NQI = QB // 128
        m0 = A.mark()
        kThr = [A.alloc([128, S], BF16) for _ in range(2)]
        qThr = [A.alloc([128, S], BF16) for _ in range(2)]
        var = [A.alloc([128, NT, 128], BF16) for _ in range(2)]
        ones = A.alloc([128, 32], BF16)
        ptr = Ring("PT", [A.alloc([128, 2, QB], BF16) for _ in range(4)])
        otr = Ring("OTs", [A.alloc([128, 2, QB], F32) for _ in range(2)])
        rwr = Ring("RSs", [A.alloc([128, QB], F32) for _ in range(2)])
        rcr = Ring("rc", [A.alloc([128, NQI, 2], F32) for _ in range(2)])
        rtr = Ring("rt2", [A.alloc([128, NQI, 2], F32) for _ in range(2)])
        nlr = Ring("nl", [A.alloc([128, NQI], F32) for _ in range(2)])
        o0r = Ring("o0", [A.alloc([128, 128], F32) for _ in range(4)])
        oor = Ring("oo", [A.alloc([128, 128], F32) for _ in range(4)])
        ssr = Ring("ss2", [A.alloc([128, 1], F32) for _ in range(4)])
        rsr = Ring("rs2", [A.alloc([128, 1], F32) for _ in range(4)])
        mxr = Ring("mx", [A.alloc([128, NQI, 128], BF16) for _ in range(2)])
        jkr = Ring("junk2", [A.alloc([128, 128], BF16) for _ in range(3)])
        self.memset("pool", ones, 1.0, W=["ones2"])
        vS3 = self.vS.rearrange("(n p) c -> p n c", p=128)
        mix3 = self.mixS.rearrange("(n p) c -> p n c", p=128)
        NG = (NT + 7) // 8
        OTP = self.psall[:, 4 * 512:6 * 512].rearrange("p (m c) -> p m c", m=2)
        TPr = self.bank(7).rearrange("p (a b) -> p a b", a=NQI)
        TPo = self.bank(7).rearrange("p (a b) -> p a b", a=4)
        X7 = [("B", 7)]
        offs = (2, 6, 16, 20) if NT >= 32 else (1, 2, 4, 5)
        pend = []
        gstep = [0]

        def defer(dt, fn):
            pend.append((gstep[0] + dt, fn))

        def run_pending(force=False):
            while pend and (force or pend[0][0] <= gstep[0]):
                pend.pop(0)[1]()

        def load_head(h):
            j = h % 2
            self.dma(kThr[j], self.kT[h * 128:(h + 1) * 128, :], W=[("kTh", j)], key=("kTh", j))
            self.dma(qThr[j], self.qT[h * 128:(h + 1) * 128, :], W=[("qTh", j)], key=("qTh", j))
            for g in range(NG):
                n0, n1 = g * 8, min(NT, g * 8 + 8)
                self.dma(var[j][:, n0:n1, :], vS3[:, n0:n1, h * 128:(h + 1) * 128],
                         W=[("va", j, g)], key=("va", j, g))

        def epilogue(h, qb):
            OT, otk = otr.next()
            self.cp("dve", OT, OTP, R=[("OT", 0), ("OT", 1)], W=[otk], X=[("B", 4), ("B", 5)])
            RW, rwk = rwr.next()
            self.cp("dve", RW, self.bank(6)[:, 0:QB], R=[("RS", g) for g in range(4)], W=[rwk], X=[("B", 6)])
            rc, rck = rcr.next()
            nl, nlk = nlr.next()
            MX, mxk = mxr.next()

            def e2():
                for qi in range(NQI):
                    self.tr(TPr[:, qi, :], RW[:, qi * 128:(qi + 1) * 128], R=[rwk, ("c", 0)], W=["TP7"],
                            X=X7, ident=self.identf)
                rt2, rtk = rtr.next()
                self.cp("dve", rt2, TPr[:, :, 0:64:32], R=["TP7"], W=[rtk], X=X7)
                self.tt("dve", rt2, rt2, TPr[:, :, 64:128:32], ALU.add, R=["TP7", rtk], W=[rtk], X=X7)
                self.recip(rc, rt2, R=[rtk], W=[rck])
                self.ts("dve", nl, rc[:, :, 1], self.lamneg[:, 0:1], None, ALU.mult, R=[rck, "lamneg"], W=[nlk])

            def e34(half):
                def fn():
                    for q2 in range(2):
                        qi = 2 * half + q2
                        for m in range(2):
                            self.tr(TPo[:, 2 * q2 + m, :], OT[:, m, qi * 128:(qi + 1) * 128], R=[otk, ("c", 0)],
                                    W=["TP7"], X=X7, ident=self.identf)
                    for q2 in range(2):
                        qi = 2 * half + q2
                        O0, o0k = o0r.next()
                        self.ts("dve", O0, TPo[:, 2 * q2, :], rc[:, qi, 0:1], None, ALU.mult, R=["TP7", rck], W=[o0k], X=X7)
                        OO, ook = oor.next()
                        self.stt(OO, TPo[:, 2 * q2 + 1, :], nl[:, qi:qi + 1], O0, ALU.mult, ALU.add,
                                 R=["TP7", nlk, o0k], W=[ook], X=X7)
                        ss, ssk = ssr.next()
                        rs, rsk = rsr.next()
                        junk, jk = jkr.next()
                        self.stt(junk, OO, 1.0, OO, ALU.mult, ALU.mult, R=[ook], W=[ssk, jk], accum=ss)
                        self.rstd_from_ss(rs, ss, 128, R=[ssk], W=[rsk], tmpkey=("rstmp2", rsk))
                        self.stt(MX[:, qi, :], OO, rs[:, 0:1], self.GS, ALU.mult, ALU.mult, R=[ook, rsk, ("c", 5)],
                                 W=[(mxk, qi)])
                return fn

            def e5():
                self.dma(mix3[:, qb * NQI:(qb + 1) * NQI, h * 128:(h + 1) * 128], MX,
                         R=[(mxk, qi) for qi in range(NQI)], W=[("mixA", h, qb)], key=mxk)

            defer(offs[0], e2)
            defer(offs[1], e34(0))
            defer(offs[2], e34(1))
            defer(offs[3], e5)

        def head(h):
            j = h % 2
            kTh, qTh, va = kThr[j], qThr[j], var[j]
            steps = [(qb, kt) for qb in range(NQB) for kt in range(NT)]
            n = len(steps)
            rs_prev = [None]
            rs_pending = []

            def QK(i):
                qb, kt = steps[i]
                s = i % 2
                for m in range(2):
                    self.mm(self.bank(2 * s + m), kTh[64 * m:64 * m + 64, kt * 128:(kt + 1) * 128],
                            qTh[64 * m:64 * m + 64, qb * QB:(qb + 1) * QB],
                            R=[("kTh", j), ("qTh", j)], W=[("ST", s, m)], X=[("B", 2 * s + m)])

            def EXP(i):
                s = i % 2
                PT, pk = ptr.next()
                src = self.psall[:, 2 * s * 512:2 * s * 512 + 1024].rearrange("p (m c) -> p m c", m=2)
                self.act(PT, src, AF.Exp, R=[("ST", s, 0), ("ST", s, 1)], W=[pk], scale=0.125,
                         X=[("B", 2 * s), ("B", 2 * s + 1)])
                return PT, pk

            def AV(i, PT, pk):
                qb, kt = steps[i]
                for m in range(2):
                    self.mm(self.bank(4 + m), va[:, kt, :], PT[:, m, :], start=(kt == 0), stop=(kt == NT - 1),
                            R=[pk, ("va", j, kt // 8)], W=[("OT", m)], X=[("B", 4 + m)])
                if kt % 2 == 0:
                    rs_prev[0] = (PT, pk)
                else:
                    PTp, pkp = rs_prev[0]

                    def quad():
                        for g, (PTx, pkx, m) in enumerate(((PTp, pkp, 0), (PTp, pkp, 1), (PT, pk, 0), (PT, pk, 1))):
                            self.mm(self.bank(6)[32 * g:32 * g + 32, 0:QB], ones, PTx[:, m, :], start=(kt == 1),
                                    stop=(kt == NT - 1), R=[pkx, "ones2"], W=[("RS", g)], X=[("B", 6)],
                                    tile_position=(0, 32 * g), skip_group_check=True)
                    if kt == NT - 1:
                        quad()
                    else:
                        rs_pending.append(quad)

            QK(0)
            for i in range(n):
                if i + 1 < n:
                    QK(i + 1)
                while rs_pending:
                    rs_pending.pop(0)()
                PT, pk = EXP(i)
                AV(i, PT, pk)
                if steps[i][1] == NT - 1:
                    epilogue(h, steps[i][0])
                run_pending()
                gstep[0] += 1

        load_head(0)
        for h in range(4):
            if h + 1 < 4:
                load_head(h + 1)
            head(h)
        run_pending(force=True)
        self.P.barrier()
        A.release(m0)

    def phase2b(self):
        A = self.arena
        S, NT, NCH = self.S, self.NT, self.NCH
        NB = NCH // 8
        m0 = A.mark()
        ST32 = [A.alloc([128, 128], F32) for _ in range(4)]
        STB = [A.alloc([128, 128], BF16) for _ in range(4)]
        QTt = [Ring(f"QTt{c}", [A.alloc([128, 512], BF16) for _ in range(2)]) for c in range(4)]
        KTt = [Ring(f"KTt{c}", [A.alloc([128, 512], BF16) for _ in range(2)]) for c in range(4)]
        KDt = [Ring(f"KDt{c}", [A.alloc([64, 8, 128], BF16) for _ in range(2)]) for c in range(4)]
        VVt = [Ring(f"VVt{c}", [A.alloc([64, 8, 128], BF16) for _ in range(2)]) for c in range(4)]
        OBt = [Ring(f"OB{c}", [A.alloc([64, 8, 128], F32) for _ in range(2)]) for c in range(4)]
        OFt = [Ring(f"OF{c}", [A.alloc([64, 8, 128], F32) for _ in range(2)]) for c in range(4)]
        GTt = [Ring(f"GT{c}", [A.alloc([64, 8, 128], F32) for _ in range(2)]) for c in range(4)]
        AMt = [Ring(f"AM{c}", [A.alloc([64, 4, 64], BF16) for _ in range(2)]) for c in range(4)]
        SQ = Ring("SQ", [A.alloc([64, 8, 128], F32) for _ in range(2)])
        ssr = Ring("ss3", [A.alloc([64, 8], F32) for _ in range(2)])
        MXH = Ring("MXH", [A.alloc([64, 8, 128], BF16) for _ in range(2)])
        hkd3 = [self.hkd[d].rearrange("(c s) k -> s c k", s=64) for d in range(2)]
        hv3 = self.hvS.rearrange("(c s) k -> s c k", s=64)
        of3 = self.ofS.rearrange("(c s) k -> s c k", s=64)
        gt3 = self.gateS.rearrange("(c s) k -> s c k", s=64)
        mixh3 = self.mixS.rearrange("(c s) f -> s c f", s=64)

        for d in range(2):
            order = list(range(NB)) if d == 0 else list(range(NB - 1, -1, -1))
            cur = {}

            def load(c, b, d=d):
                h = c
                t0 = b * 512
                hs = slice(h * 128, (h + 1) * 128)
                q, qk = QTt[c].next()
                k, kk = KTt[c].next()
                kd, kdk = KDt[c].next()
                vv, vk = VVt[c].next()
                self.dma(q, self.hqT[d, hs, t0:t0 + 512], W=[qk], key=qk)
                self.dma(k, self.hkT[d, hs, t0:t0 + 512], W=[kk], key=kk)
                self.dma(kd, hkd3[d][:, 8 * b:8 * b + 8, hs], W=[kdk], key=kdk)
                self.dma(vv, hv3[:, 8 * b:8 * b + 8, hs], W=[vk], key=vk)
                ob, obk = OBt[c].next()
                ent = dict(q=(q, qk), k=(k, kk), kd=(kd, kdk), vv=(vv, vk), ob=(ob, obk))
                if d == 1:
                    of, ofk = OFt[c].next()
                    gt, gtk = GTt[c].next()
                    self.dma(of, of3[:, 8 * b:8 * b + 8, hs], W=[ofk], key=ofk)
                    self.dma(gt, gt3[:, 8 * b:8 * b + 8, hs], W=[gtk], key=gtk)
                    self.tt("pool", gt, gt, self.GHG[0:64, :].unsqueeze(1).to_broadcast([64, 8, 128]), ALU.mult,
                            R=[gtk], W=[gtk])
                    ent["of"] = (of, ofk)
                    ent["gt"] = (gt, gtk)
                cur[(c, b)] = ent

            for c in range(4):
                self.memset("pool", ST32[c], 0.0, W=[("ST32", c)])
                self.memset("pool", STB[c], 0.0, W=[("STB", c)])
                load(c, order[0])
            for bi, b in enumerate(order):
                if bi + 1 < NB:
                    for c in range(4):
                        load(c, order[bi + 1])
                subs = [(0, 4), (4, 8)] if d == 0 else [(4, 8), (0, 4)]
                for (j0, j1) in subs:
                    js = list(range(j0, j1)) if d == 0 else list(range(j1 - 1, j0 - 1, -1))
                    ams = {}
                    for c in range(4):
                        e = cur[(c, b)]
                        q, qk = e["q"]
                        k, kk = e["k"]
                        XA = [("B", 2 * c)]
                        pa = self.bank(2 * c)[0:64, 0:256].rearrange("p (a b) -> p a b", a=4)
                        for jj in range(4):
                            j = j0 + jj
                            self.mm(pa[:, jj, :], k[:, j * 64:(j + 1) * 64], q[:, j * 64:(j + 1) * 64],
                                    R=[qk, kk], W=[("pa", c)], X=XA)
                        am, amk = AMt[c].next()
                        self.tt("dve", am, pa, self.mask[:, d:d + 1, :].to_broadcast([64, 4, 64]), ALU.mult,
                                R=[("pa", c)], W=[amk], X=XA)
                        ams[c] = (am, amk)
                    for j in js:
                        for c in range(4):
                            e = cur[(c, b)]
                            q, qk = e["q"]
                            kd, kdk = e["kd"]
                            vv, vk = e["vv"]
                            ob, obk = e["ob"]
                            am, amk = ams[c]
                            XA = [("B", 2 * c)]
                            XU = [("B", 2 * c + 1)]
                            po = self.bank(2 * c)[0:64, 256:384]
                            pu = self.bank(2 * c + 1)[:, 0:128]
                            ch = 8 * b + j
                            self.mm(po, q[:, j * 64:(j + 1) * 64], STB[c], start=True, stop=False,
                                    R=[qk, ("STB", c)], W=[("po", c)], X=XA)
                            self.mm(po, am[:, j - j0, :], vv[:, j, :], start=False, stop=True,
                                    R=[amk, vk], W=[("po", c)], X=XA)
                            self.mm(pu, kd[:, j, :], vv[:, j, :], R=[kdk, vk], W=[("pu", c)], X=XU)
                            self.cp("act", ob[:, j, :], po, R=[("po", c)], W=[obk], X=XA)
                            self.stt(ST32[c], ST32[c], self.EDEC[:, d, c, ch:ch + 1], pu, ALU.mult, ALU.add,
                                     R=[("ST32", c), ("pu", c)], W=[("ST32", c)], X=XU)
                            self.cp("act", STB[c], ST32[c], R=[("ST32", c)], W=[("STB", c)])
                for c in range(4):
                    e = cur.pop((c, b))
                    ob, obk = e["ob"]
                    hs = slice(c * 128, (c + 1) * 128)
                    if d == 0:
                        self.dma(of3[:, 8 * b:8 * b + 8, hs], ob, R=[obk], W=[("ofS", c, b)], key=obk)
                    else:
                        of, ofk = e["of"]
                        gt, gtk = e["gt"]
                        self.tt("dve", ob, ob, of, ALU.add, R=[obk, ofk], W=[obk])
                        sq, sqk = SQ.next()
                        ss, ssk = ssr.next()
                        for j in range(8):
                            self.stt(sq[:, j, :], ob[:, j, :], 1.0, ob[:, j, :], ALU.mult, ALU.mult, R=[obk],
                                     W=[(ssk, j), (sqk, j)], accum=ss[:, j:j + 1])
                        self.rstd_from_ss(ss, ss, 128, R=[(ssk, j) for j in range(8)], W=[ssk], tmpkey=("rstmp3", ssk))
                        self.tt("dve", ob, ob, ss.unsqueeze(2).to_broadcast([64, 8, 128]), ALU.mult, R=[obk, ssk], W=[obk])
                        mx, mxk = MXH.next()
                        self.tt("dve", mx, ob, gt, ALU.mult, R=[obk, gtk], W=[mxk])
                        self.dma(mixh3[:, 8 * b:8 * b + 8, 512 + c * 128:512 + (c + 1) * 128], mx, R=[mxk],
                                 W=[("mixH", c, b)], key=mxk)
            self.P.barrier()
        A.release(m0)

    def phase3(self):
        A = self.arena
        S, NT = self.S, self.NT
        self.m_ffn = A.mark()
        self.WG = A.alloc([128, 8, DFF], BF16)
        self.WU = A.alloc([128, 8, DFF], BF16)
        self.WD = A.alloc([128, NFC, D], BF16)
        m3 = A.mark()
        WOUT = A.alloc([128, 8, D], BF16)
        wst = Ring("wst3", [A.alloc([128, 704], F32) for _ in range(3)])
        wl = []
        for kc in range(8):
            for cb in range(2):
                wl.append((self.w_out[kc * 128:(kc + 1) * 128, cb * 512:(cb + 1) * 512],
                           WOUT[:, kc, cb * 512:(cb + 1) * 512], None, ("WOUT", kc)))
        n_wout = len(wl)
        for wi, (wd, WS) in enumerate(((self.w_gate, self.WG), (self.w_up, self.WU))):
            for kc in range(8):
                for cb in range(4):
                    wl.append((wd[kc * 128:(kc + 1) * 128, cb * 704:(cb + 1) * 704],
                               WS[:, kc, cb * 704:(cb + 1) * 704], self.gpk[:, 1, kc:kc + 1], ("WGU", wi, kc, cb)))
        for fc in range(NFC):
            for cb in range(2):
                wl.append((self.w_down[fc * 128:(fc + 1) * 128, cb * 512:(cb + 1) * 512],
                           self.WD[:, fc, cb * 512:(cb + 1) * 512], None, ("WD", fc, cb)))

        def wload(i):
            src, dst, sc, res = wl[i]
            st, k = wst.next()
            w = src.shape[1]
            self.dma(st[:, 0:w], src, W=[k], key=k)
            ce = "pool" if i % 2 == 0 else "dve"
            if sc is None:
                self.cp(ce, dst, st[:, 0:w], R=[k], W=[res])
            else:
                self.ts(ce, dst, st[:, 0:w], sc, 0.0, ALU.mult, ALU.add, R=[k], W=[res])

        mxt = Ring("mxt", [A.alloc([128, D], BF16) for _ in range(2)])
        xtr = Ring("xt3", [A.alloc([128, D], F32) for _ in range(2)])
        mTr = Ring("mT", [A.alloc([128, 8, 128], BF16) for _ in range(2)])
        x1r = Ring("x1", [A.alloc([128, D], F32) for _ in range(2)])
        xnr = Ring("xn3", [A.alloc([128, D], BF16) for _ in range(2)])
        h2s = Ring("h2s", [A.alloc([128, 8, 256], BF16) for _ in range(2)])
        ssr = Ring("ss4", [A.alloc([128, 1], F32) for _ in range(4)])
        rsr = Ring("rs4", [A.alloc([128, 1], F32) for _ in range(4)])
        jkr = Ring("junk3", [A.alloc([128, D], BF16) for _ in range(2)])
        h2T3 = self.h2T.rearrange("(kc p) t -> p kc t", p=128)
        zz = self.zero[:, 0:8].rearrange("p (a b) -> p a b", b=1)
        self.dma(h2T3[:, :, 0:1], zz, W=[("h2T", "z0")], key="z0", slow=True)
        self.dma(h2T3[:, :, S + 1:S + 2], zz, W=[("h2T", "z1")], key="z1", slow=True)
        stg_state = {}
        nwc = [0]
        while nwc[0] < n_wout:
            wload(nwc[0])
            nwc[0] += 1
        st3 = {}

        def g0(n):
            mx, mxk = mxt.next()
            self.dma(mx, self.mixS[n * 128:(n + 1) * 128, :], W=[mxk], key=mxk)
            while nwc[0] < len(wl) and nwc[0] < n_wout + (n + 1) * (len(wl) - n_wout) // max(1, NT) + 1:
                wload(nwc[0])
                nwc[0] += 1
            st3[n] = dict(mx=(mx, mxk))

        def g1(n):
            mx, mxk = st3[n]["mx"]
            XT = [("B", 0)]
            tp = self.bank_bf(0)
            for kc in range(8):
                self.tr(tp[:, kc, :], mx[:, kc * 128:(kc + 1) * 128], R=[mxk], W=["tp3"], X=XT)
            mT, mk = mTr.next()
            self.cp("act", mT, tp, R=["tp3"], W=[mk], X=XT)
            st3[n]["mT"] = (mT, mk)

        def g2(n):
            mT, mk = st3[n]["mT"]
            x_t, xk = xtr.next()
            self.dma(x_t, self.x[n * 128:(n + 1) * 128, :], W=[xk], key=xk)
            s = n % 2
            po = self.bank(1 + 2 * s, 2)
            for hf in range(2):
                for kc in range(8):
                    self.mm(po[:, hf * 512:(hf + 1) * 512], mT[:, kc, :], WOUT[:, kc, hf * 512:(hf + 1) * 512],
                            start=(kc == 0), stop=(kc == 7), R=[mk, ("WOUT", kc)], W=[("po3", s)], X=[("B", 1 + 2 * s + hf)])
            st3[n]["x"] = (x_t, xk)

        def g3(n):
            x_t, xk = st3[n]["x"]
            s = n % 2
            po = self.bank(1 + 2 * s, 2)
            XO = [("B", 1 + 2 * s), ("B", 2 + 2 * s)]
            x1, x1k = x1r.next()
            self.tt("dve", x1, po, x_t, ALU.add, R=[("po3", s), xk], W=[x1k], X=XO)
            self.dma(self.x1S[n * 128:(n + 1) * 128, :], x1, R=[x1k], W=[("x1S", n)], key=x1k)
            ss, ssk = ssr.next()
            rs, rsk = rsr.next()
            junk, jk = jkr.next()
            self.stt(junk, x1, 1.0, x1, ALU.mult, ALU.mult, R=[x1k], W=[ssk, jk], accum=ss)
            self.rstd_from_ss(rs, ss, D, R=[ssk], W=[rsk], tmpkey=("rstmp4", rsk))
            xn, xnk = xnr.next()
            self.act(xn, x1, AF.Copy, R=[x1k, rsk], W=[xnk], scale=rs[:, 0:1])
            st3[n]["xn"] = (xn, xnk)

        def g4(n):
            xn, xnk = st3.pop(n)["xn"]
            XT5 = [("B", 5)]
            tp5 = self.bank_bf(5)
            for kc in range(8):
                self.tr(tp5[:, kc, :], xn[:, kc * 128:(kc + 1) * 128], R=[xnk], W=["tp5"], X=XT5)
            j = n % 2
            if j == 0:
                stg_state["h2"] = h2s.next()
            stg, sk = stg_state["h2"]
            self.cp("dve", stg[:, :, j * 128:(j + 1) * 128], tp5, R=["tp5"], W=[sk], X=XT5)
            if j == 1 or n == NT - 1:
                n0 = n - j
                w = (j + 1) * 128
                self.dma(h2T3[:, :, 1 + n0 * 128:1 + n0 * 128 + w], stg[:, :, 0:w], R=[sk], W=[("h2T", n0)], key=sk)

        skew([g0, g1, g2, g3, g4], NT)
        nw = nwc[0]
        while nw < len(wl):
            wload(nw)
            nw += 1
        self.P.barrier()
        A.release(m3)

    def phase4(self):
        A = self.arena
        S, NT = self.S, self.NT
        NSB = S // 512
        m4 = A.mark()
        CP = A.alloc([128, 4, NFC], F32)
        self.dma(CP, self.convp, W=["CP"], key="CP")
        GU = A.alloc([128, NFC, 512], BF16)
        HTr = Ring("HT", [A.alloc([128, 8, 514], BF16) for _ in range(2)])
        Asr = Ring("Asb", [A.alloc([128, 514], F32) for _ in range(2)])
        Cr = Ring("Cc", [A.alloc([128, 512], F32) for _ in range(2)])
        Gr = Ring("Gg", [A.alloc([128, 512], F32) for _ in range(2)])
        x1r = Ring("x1b", [A.alloc([128, D], F32) for _ in range(2)])
        x2r = Ring("x2b", [A.alloc([128, D], F32) for _ in range(2)])
        h2T3 = self.h2T.rearrange("(kc p) t -> p kc t", p=128)

        def loadHT(sb):
            HT, hk = HTr.next()
            self.dma(HT, h2T3[:, :, sb * 512:sb * 512 + 514], W=[hk], key=hk)
            return HT, hk

        nxt = loadHT(0)
        for sb in range(NSB):
            HT, hk = nxt
            if sb + 1 < NSB:
                nxt = loadHT(sb + 1)
            for fc in range(NFC):
                s = fc % 2
                pa = self.bank(s)
                pu = self.bank(2 + s)
                ph = self.bank(4)[:, 2 * s:2 * s + 2]
                fs = slice(fc * 128, (fc + 1) * 128)
                for kc in range(8):
                    self.mm(pa, self.WG[:, kc, fs], HT[:, kc, 1:513], start=(kc == 0), stop=(kc == 7),
                            R=[hk], W=[("pa4", s)], X=[("B", s)])
                for kc in range(8):
                    self.mm(ph, self.WG[:, kc, fs], HT[:, kc, 0:514:513], start=(kc == 0), stop=(kc == 7),
                            R=[hk], W=[("ph4", s)], X=[("B", 4)])
                for kc in range(8):
                    self.mm(pu, self.WU[:, kc, fs], HT[:, kc, 1:513], start=(kc == 0), stop=(kc == 7),
                            R=[hk], W=[("pu4", s)], X=[("B", 2 + s)])
                Asb, ak = Asr.next()
                self.cp("act", Asb[:, 1:513], pa, R=[("pa4", s)], W=[(ak, "m")], X=[("B", s)])
                self.cp("act", Asb[:, 0:514:513], ph, R=[("ph4", s)], W=[(ak, "h")], X=[("B", 4)])
                C, ck = Cr.next()
                self.ts("pool", C, Asb[:, 1:513], CP[:, 1, fc:fc + 1], CP[:, 3, fc:fc + 1], ALU.mult, ALU.add,
                        R=[(ak, "m"), "CP"], W=[ck])
                self.stt(C, Asb[:, 0:512], CP[:, 0, fc:fc + 1], C, ALU.mult, ALU.add, R=[(ak, "m"), (ak, "h"), ck], W=[ck])
                self.stt(C, Asb[:, 2:514], CP[:, 2, fc:fc + 1], C, ALU.mult, ALU.add, R=[(ak, "m"), (ak, "h"), ck], W=[ck])
                G, gk = Gr.next()
                self.act(G, C, AF.Gelu, R=[ck], W=[gk])
                self.tt("dve", GU[:, fc, :], G, pu, ALU.mult, R=[gk, ("pu4", s)], W=[("GU", fc)], X=[("B", 2 + s)])
            for ti in range(4):
                n = sb * 4 + ti
                s = ti % 2
                x1, x1k = x1r.next()
                self.dma(x1, self.x1S[n * 128:(n + 1) * 128, :], W=[x1k], key=x1k)
                x2, x2k = x2r.next()
                for hf in range(2):
                    bk = 6 + hf
                    pdh = self.bank(bk)
                    for fc in range(NFC):
                        self.mm(pdh, GU[:, fc, ti * 128:(ti + 1) * 128], self.WD[:, fc, hf * 512:(hf + 1) * 512],
                                start=(fc == 0), stop=(fc == NFC - 1), R=[("GU", fc)], W=[("pd4", hf)], X=[("B", bk)])
                    self.tt("dve", x2[:, hf * 512:(hf + 1) * 512], pdh, x1[:, hf * 512:(hf + 1) * 512], ALU.add,
                            R=[("pd4", hf), x1k], W=[(x2k, hf)], X=[("B", bk)])
                self.dma(self.x2S[n * 128:(n + 1) * 128, :], x2, R=[(x2k, 0), (x2k, 1)], W=[("x2S", n)], key=x2k)
        self.P.barrier()
        A.release(self.m_ffn)

    def phase5(self):
        A = self.arena
        S, NT = self.S, self.NT
        m5 = A.mark()
        WPG = A.alloc([128, 8, D], BF16)
        WPL = A.alloc([128, 2, D], BF16)
        GF = A.alloc([128, D], F32)
        wst = Ring("wst5", [A.alloc([128, D], F32) for _ in range(2)])
        for kc in range(8):
            st, k = wst.next()
            self.dma(st, self.w_pleg[kc * 128:(kc + 1) * 128, :], W=[k], key=k)
            self.ts("pool" if kc % 2 else "dve", WPG[:, kc, :], st, self.gpk[:, 2, kc:kc + 1], 0.0, ALU.mult, ALU.add,
                    R=[k], W=[("WPG", kc)])
        for kc in range(2):
            st, k = wst.next()
            self.dma(st, self.w_ple[kc * 128:(kc + 1) * 128, :], W=[k], key=k)
            self.cp("pool", WPL[:, kc, :], st, R=[k], W=[("WPL", kc)])
        self.dma(GF, self.gfin, W=["GF"], key="GF")
        x2r = Ring("x2c", [A.alloc([128, D], F32) for _ in range(8)])
        ptr = Ring("pt5", [A.alloc([128, 256], F32) for _ in range(3)])
        pbr = Ring("pb5", [A.alloc([128, 256], BF16) for _ in range(4)])
        xnr = Ring("xn5", [A.alloc([128, D], BF16) for _ in range(3)])
        h3r = Ring("h3", [A.alloc([128, 8, 128], BF16) for _ in range(3)])
        pTr = Ring("pT", [A.alloc([128, 2, 128], BF16) for _ in range(4)])
        thr = Ring("th", [A.alloc([128, D], F32) for _ in range(3)])
        x3r = Ring("x3", [A.alloc([128, D], F32) for _ in range(4)])
        outr = Ring("outt", [A.alloc([128, D], F32) for _ in range(3)])
        ssr = Ring("ss5", [A.alloc([128, 1], F32) for _ in range(8)])
        rsr = Ring("rs5", [A.alloc([128, 1], F32) for _ in range(8)])
        junk = A.alloc([128, D], BF16)
        st5 = {}

        def fl(n):
            x2, x2k = x2r.next()
            pt, ptk = ptr.next()
            self.dma(x2, self.x2S[n * 128:(n + 1) * 128, :], W=[x2k], key=x2k)
            self.dma(pt, self.p_in[n * 128:(n + 1) * 128, :], W=[ptk], key=ptk)
            st5[n] = dict(x2=(x2, x2k), pt=(pt, ptk))

        def f0(n):
            e = st5[n]
            x2, x2k = e["x2"]
            pt, ptk = e["pt"]
            pb, pbk = pbr.next()
            self.cp("pool", pb, pt, R=[ptk], W=[pbk])
            ss, ssk = ssr.next()
            rs, rsk = rsr.next()
            self.act(junk, x2, AF.Square, R=[x2k], W=[ssk, "junk5"], accum=ss)
            self.rstd_from_ss(rs, ss, D, R=[ssk], W=[rsk], tmpkey=("rstmp5", rsk))
            e["pb"] = (pb, pbk)
            e["rs"] = (rs, rsk)

        def f0c(n):
            e = st5[n]
            x2, x2k = e["x2"]
            rs, rsk = e["rs"]
            xn, xnk = xnr.next()
            self.act(xn, x2, AF.Copy, R=[x2k, rsk], W=[xnk], scale=rs[:, 0:1])
            e["xn"] = (xn, xnk)

        def f1(n):
            e = st5[n]
            xn, xnk = e["xn"]
            pb, pbk = e["pb"]
            s = n % 2
            tb = 4 * s
            XT = [("B", tb)]
            tp = self.bank_bf(tb)
            for kc in range(8):
                self.tr(tp[:, kc, :], xn[:, kc * 128:(kc + 1) * 128], R=[xnk], W=[("tp5a", s)], X=XT)
            h3, h3k = h3r.next()
            self.cp("act", h3, tp, R=[("tp5a", s)], W=[h3k], X=XT)
            XP = [("B", tb + 1)]
            tq = self.bank_bf(tb + 1)
            for kc in range(2):
                self.tr(tq[:, kc, :], pb[:, kc * 128:(kc + 1) * 128], R=[pbk], W=[("tp5b", s)], X=XP)
            pT, pTk = pTr.next()
            self.cp("dve", pT, tq[:, 0:2, :], R=[("tp5b", s)], W=[pTk], X=XP)
            e["h3"] = (h3, h3k)
            e["pT"] = (pT, pTk)

        def f2(n):
            e = st5[n]
            h3, h3k = e["h3"]
            s = n % 2
            tb = 4 * s
            pg = self.bank(tb + 2, 2)
            XG = [("B", tb + 2), ("B", tb + 3)]
            for hf in range(2):
                for kc in range(8):
                    self.mm(pg[:, hf * 512:(hf + 1) * 512], h3[:, kc, :], WPG[:, kc, hf * 512:(hf + 1) * 512],
                            start=(kc == 0), stop=(kc == 7), R=[h3k, ("WPG", kc)], W=[("pg5", s)], X=[("B", tb + 2 + hf)])
            th, thk = thr.next()
            self.act(th, pg, AF.Tanh, R=[("pg5", s)], W=[thk], scale=0.5, X=XG)
            e["th"] = (th, thk)

        def f3(n):
            e = st5[n]
            pT, pTk = e["pT"]
            th, thk = e["th"]
            x2, x2k = e["x2"]
            s = n % 2
            tb = 4 * s
            pg = self.bank(tb + 2, 2)
            XG = [("B", tb + 2), ("B", tb + 3)]
            for hf in range(2):
                for kc in range(2):
                    self.mm(pg[:, hf * 512:(hf + 1) * 512], pT[:, kc, :], WPL[:, kc, hf * 512:(hf + 1) * 512],
                            start=(kc == 0), stop=(kc == 1), R=[pTk, ("WPL", kc)], W=[("pg5", s)], X=[("B", tb + 2 + hf)])
            x3, x3k = x3r.next()
            self.stt(x3, th, 1.0, pg, ALU.add, ALU.mult, R=[("pg5", s), thk], W=[x3k], X=XG)
            self.stt(x3, x3, 0.5, x2, ALU.mult, ALU.add, R=[x3k, x2k], W=[x3k])
            e["x3"] = (x3, x3k)

        def f4(n):
            e = st5[n]
            x3, x3k = e["x3"]
            ss, ssk = ssr.next()
            rs, rsk = rsr.next()
            self.act(junk, x3, AF.Square, R=[x3k], W=[ssk, "junk5"], accum=ss)
            self.rstd_from_ss(rs, ss, D, R=[ssk], W=[rsk], tmpkey=("rstmp5", rsk))
            e["rs2"] = (rs, rsk)

        def f5(n):
            e = st5.pop(n)
            x3, x3k = e["x3"]
            rs, rsk = e["rs2"]
            ot, otk = outr.next()
            self.stt(ot, x3, rs[:, 0:1], GF, ALU.mult, ALU.mult, R=[x3k, rsk, "GF"], W=[otk])
            self.dma(self.out[n * 128:(n + 1) * 128, :], ot, R=[otk], W=[("out", n)], key=otk)

        skew([fl, f0, f0c, f1, f2, f3, f4, f5], NT)
        self.P.barrier()
        A.release(m5)

    def build(self):
        self.setup()
        if 1 in self.phases:
            self.phase1()
        if 2 in self.phases:
            self.phase2a()
            self.phase2b()
        if 3 in self.phases:
            self.phase3()
        if 4 in self.phases:
            self.phase4()
        if 5 in self.phases:
            self.phase5()
        self.P.barrier()
        self.P.emit()
        return self.nc


def host_consts():
    c = {}
    c["c_ident"] = np.eye(128, dtype=np.float32)
    s = np.arange(128)[:, None]
    t = np.arange(128)[None, :]
    same = (s // 64) == (t // 64)
    tri = np.zeros((128, 4, 128), np.float32)
    tri[:, 0, :] = same & (s <= t)
    tri[:, 1, :] = same & (s > t)
    tri[:, 2, :] = same & (s >= t)
    tri[:, 3, :] = same & (s < t)
    c["c_tri"] = tri
    ind = np.zeros((128, 2), np.float32)
    ind[:64, 0] = 1
    ind[64:, 1] = 1
    c["c_ind"] = ind
    s = np.arange(64)[:, None]
    t = np.arange(64)[None, :]
    mk = np.zeros((64, 2, 64), np.float32)
    mk[:, 0, :] = s <= t
    mk[:, 1, :] = s >= t
    c["c_mask"] = mk
    half = 8
    invf = (500000.0 ** (-np.arange(half, dtype=np.float32) / half)).astype(np.float32)
    c["c_invf"] = np.ascontiguousarray(np.broadcast_to(invf[None, :], (128, 8))).astype(np.float32)
    return c


def host_params(inp, S):
    f = lambda a: np.ascontiguousarray(np.asarray(a), dtype=np.float32)
    m = {}
    m["w_in"] = f(inp["w_in"][0])
    m["w_out"] = f(inp["w_out"][0])
    m["w_gate"] = f(inp["w_ffn_gate"][0])
    m["w_up"] = f(inp["w_ffn_up"][0])
    m["w_down"] = f(inp["w_ffn_down"][0])
    m["w_ple"] = f(inp["w_ple"][0])
    m["w_pleg"] = f(inp["w_ple_gate"][0])
    g3 = np.stack([np.asarray(inp["norm_mix_g"][0]), np.asarray(inp["norm_ffn_g"][0]),
                   np.asarray(inp["norm_ple_g"][0])], 0)
    m["g_pk"] = f(g3.reshape(3, 8, 128).transpose(2, 0, 1))
    rep = lambda a, shape: f(np.broadcast_to(np.asarray(a)[None], (128,) + tuple(shape)))
    m["gam"] = rep(inp["hg_lb_gamma"], (2, 2, 512))
    lv = np.stack([np.asarray(inp["lam_q1"][0]), np.asarray(inp["lam_k1"][0]),
                   np.asarray(inp["lam_q2"][0]), np.asarray(inp["lam_k2"][0])], 0)
    m["lamv"] = rep(lv, (4, 64))
    m["gsub"] = rep(inp["da_subln_g"][0], (128,))
    m["ghg"] = rep(inp["hg_norm_g"][0], (128,))
    m["gfin"] = rep(inp["final_norm_g"], (D,))
    cw = np.concatenate([np.asarray(inp["ffn_conv_w"][0]), np.asarray(inp["ffn_conv_b"][0])[None]], 0)
    m["convp"] = f(cw.reshape(4, NFC, 128).transpose(2, 0, 1))
    m.update(host_consts())
    return m


def core_inputs(inp, b, S, shared):
    m = dict(shared)
    m["x"] = np.ascontiguousarray(np.asarray(inp["x"][b], dtype=np.float32))
    m["p"] = np.ascontiguousarray(np.asarray(inp["p"][0, b], dtype=np.float32))
    pos = np.asarray(inp["positions"][b]).astype(np.int32)
    m["pos"] = np.ascontiguousarray(pos.reshape(S // 128, 128).T)
    return m


_CACHE = {}


def kernel(**inputs):
    B, S = inputs["x"].shape[0], inputs["x"].shape[1]
    if S not in _CACHE:
        _CACHE[S] = KB(S).build()
    nc = _CACHE[S]
    shared = host_params(inputs, S)
    in_maps = [core_inputs(inputs, b, S, shared) for b in range(B)]
    res = run_bass_kernel_spmd(nc, in_maps, core_ids=list(range(B)))
    return np.stack([np.asarray(r["out"], dtype=np.float32) for r in res.results], 0)
```
